# Optimizing a Trainium2 kernel written in Bass

```python
import numpy as np
import jax
import jax.numpy as jnp
from jax import lax

D_MODEL = 1024
BATCH = 32
SEQ = 256
DEPTH = 2
DEC_BATCH = 8
DEC_SEQ = 4096
PAST_LEN = 256

GRID_W = 64
N_MOD = 9
FFN_DIM = 2816
BRANCH_W = 512
NA_HEADS = 8
NA_HEAD_DIM = 64
NA_WIN_H = 8
NA_WIN_W = 16
NA_QBLK = 16
NA_SLAB_W = NA_QBLK + NA_WIN_W
POOL_GROUPS = 4
POOL_DIM = 512
POOL_GROUP_DIM = POOL_DIM // POOL_GROUPS
POOL_WINDOWS = (2, 4, 8, 16)
MLA_HEADS = 8
MLA_Q_LORA = 256
MLA_KV_LORA = 256
MLA_NOPE = 64
MLA_ROPE = 32
MLA_V = 64
ROPE_BASE = 10000.0
ATTN_QBLOCK = 128
NORM_EPS = 1e-6
NEG_INF = -1e30
IN_SPLITS = (NA_HEADS * NA_HEAD_DIM, NA_HEADS * NA_HEAD_DIM, NA_HEADS * NA_HEAD_DIM, POOL_DIM, MLA_Q_LORA, MLA_KV_LORA, MLA_ROPE, 3 * D_MODEL)
IN_COLS = sum(IN_SPLITS)

kernel_name = 'hybrid_flow_na_pool_mla_step'


def rmsnorm(x, g):
    xf = x.astype(jnp.float32)
    y = xf * lax.rsqrt(jnp.mean(xf * xf, axis=-1, keepdims=True) + NORM_EPS)
    return (y * g.astype(jnp.float32)).astype(x.dtype)


def modulations(cvec, w_ada, b_ada):
    m = jax.nn.silu(cvec) @ w_ada + b_ada
    return m.reshape(cvec.shape[0], 1, N_MOD, D_MODEL)


def swiglu(u, w_in, w_out):
    gate, up = jnp.split(u @ w_in, 2, axis=-1)
    return (jax.nn.silu(gate) * up) @ w_out


def axial_rope(x, n_tok):
    half = MLA_ROPE // 2
    n_freq = half // 2
    inv = ROPE_BASE ** (-jnp.arange(n_freq, dtype=jnp.float32) / n_freq)
    t = jnp.arange(n_tok)
    bshape = (n_tok,) + (1,) * (x.ndim - 3) + (n_freq,)
    ang_r = ((t // GRID_W).astype(jnp.float32)[:, None] * inv).reshape(bshape)
    ang_c = ((t % GRID_W).astype(jnp.float32)[:, None] * inv).reshape(bshape)
    xf = x.astype(jnp.float32)

    def rot(xa, ang):
        x1, x2 = jnp.split(xa, 2, axis=-1)
        cos, sin = jnp.cos(ang), jnp.sin(ang)
        return jnp.concatenate([x1 * cos - x2 * sin, x2 * cos + x1 * sin], axis=-1)

    return jnp.concatenate([rot(xf[..., :half], ang_r), rot(xf[..., half:], ang_c)], axis=-1).astype(x.dtype)


def blocked_attention(q, k, v):
    b, nq, h, dk = q.shape
    nb = nq // ATTN_QBLOCK
    qb = q.reshape(b, nb, ATTN_QBLOCK, h, dk).transpose(1, 0, 2, 3, 4)

    def one_block(qi):
        s = jnp.einsum('bqhd,bkhd->bhqk', qi, k).astype(jnp.float32)
        p = jax.nn.softmax(s, axis=-1).astype(v.dtype)
        return jnp.einsum('bhqk,bkhd->bqhd', p, v)

    o = lax.map(one_block, qb)
    return o.transpose(1, 0, 2, 3, 4).reshape(b, nq, h, v.shape[-1])


def neighborhood_attention(q, k, v, k_ctx, v_ctx, rpb):
    b, n, h, dh = q.shape
    rows = n // GRID_W
    wr = min(NA_WIN_H, rows)
    ncb = GRID_W // NA_QBLK
    qg = q.reshape(b, rows, ncb, NA_QBLK, h, dh)
    kg = k.reshape(b, rows, GRID_W, h, dh)
    vg = v.reshape(b, rows, GRID_W, h, dh)
    row_start = jnp.clip(jnp.arange(rows) - wr // 2, 0, rows - wr)
    q_cols = jnp.arange(GRID_W).reshape(ncb, NA_QBLK)
    slab_start = jnp.clip(jnp.arange(ncb) * NA_QBLK - NA_WIN_W // 2, 0, GRID_W - NA_SLAB_W)
    slab_cols = slab_start[:, None] + jnp.arange(NA_SLAB_W)
    win_start = jnp.clip(q_cols - NA_WIN_W // 2, 0, GRID_W - NA_WIN_W)
    kc = slab_cols[:, None, :]
    in_win = (kc >= win_start[:, :, None]) & (kc < win_start[:, :, None] + NA_WIN_W)
    col_idx = jnp.clip(kc - q_cols[:, :, None] + NA_WIN_W - 1, 0, 2 * NA_WIN_W - 2)
    n_loc = wr * NA_SLAB_W

    def one_row(r):
        rs = row_start[r]
        k_blk = lax.dynamic_slice_in_dim(kg, rs, wr, axis=1)[:, :, slab_cols]
        v_blk = lax.dynamic_slice_in_dim(vg, rs, wr, axis=1)[:, :, slab_cols]
        q_blk = qg[:, r]
        s_loc = jnp.einsum('bnqhd,bwnshd->bhnqws', q_blk, k_blk).astype(jnp.float32)
        rel_r = rs + jnp.arange(wr) - r + NA_WIN_H - 1
        bias = rpb[:, rel_r[None, None, :, None], col_idx[:, :, None, :]]
        s_loc = jnp.where(in_win[:, :, None, :], s_loc + bias.astype(jnp.float32), NEG_INF)
        s_loc = s_loc.reshape(b, h, ncb, NA_QBLK, n_loc)
        s_ctx = jnp.einsum('bnqhd,bphd->bhnqp', q_blk, k_ctx).astype(jnp.float32)
        p = jax.nn.softmax(jnp.concatenate([s_loc, s_ctx], axis=-1), axis=-1).astype(v.dtype)
        v_loc = v_blk.transpose(0, 2, 1, 3, 4, 5).reshape(b, ncb, n_loc, h, dh)
        return (jnp.einsum('bhnqs,bnshd->bnqhd', p[..., :n_loc], v_loc)
                + jnp.einsum('bhnqp,bphd->bnqhd', p[..., n_loc:], v_ctx))

    o = lax.map(one_row, jnp.arange(rows))
    return o.transpose(1, 0, 2, 3, 4, 5).reshape(b, n, h, dh)


def pool_mix(p_in, pool_w, pool_scale):
    b, n, _ = p_in.shape
    xg = p_in.astype(jnp.float32).reshape(b, n, POOL_GROUPS, POOL_GROUP_DIM)
    csum = jnp.concatenate([jnp.zeros((b, 1, POOL_GROUPS, POOL_GROUP_DIM), jnp.float32), jnp.cumsum(xg, axis=1)], axis=1)
    t = jnp.arange(n)
    outs = []
    for g, w in enumerate(POOL_WINDOWS):
        lo = jnp.clip(t - w // 2, 0, n)
        hi = jnp.clip(t + w // 2, 0, n)
        cg = csum[:, :, g]
        mean = (cg[:, hi] - cg[:, lo]) / (hi - lo).astype(jnp.float32)[:, None]
        outs.append(mean - xg[:, :, g])
    y = jnp.stack(outs, axis=2).astype(p_in.dtype)
    y = jnp.einsum('bngc,gcd->bngd', y, pool_w)
    return y.reshape(b, n, POOL_DIM) * pool_scale


def split_cols(h):
    bounds = [int(i) for i in np.cumsum(IN_SPLITS)[:-1]]
    return jnp.split(h, bounds, axis=-1)


def mla_queries(cq, q_norm, w_uq):
    b, n, _ = cq.shape
    q = (rmsnorm(cq, q_norm) @ w_uq).reshape(b, n, MLA_HEADS, MLA_NOPE + MLA_ROPE)
    return q[..., :MLA_NOPE], q[..., MLA_NOPE:]


def mla_kv(ckv, w_ukv):
    b, n, _ = ckv.shape
    kv = (ckv @ w_ukv).reshape(b, n, MLA_HEADS, MLA_NOPE + MLA_V)
    return kv[..., :MLA_NOPE], kv[..., MLA_NOPE:]


def mla_keys(k_nope, krope):
    kr = jnp.broadcast_to(krope[:, :, None, :], k_nope.shape[:-1] + (MLA_ROPE,))
    return jnp.concatenate([k_nope, kr], axis=-1)


def merge_branches(na_o, pool_o, mla_o, gates, w_branch, w_out):
    b, n, _ = gates.shape
    br = jnp.stack([na_o.reshape(b, n, BRANCH_W), pool_o, mla_o.reshape(b, n, BRANCH_W)], axis=2)
    y = jnp.einsum('bnkc,kcd->bnkd', br, w_branch)
    g = jax.nn.sigmoid(gates.reshape(b, n, 3, D_MODEL))
    return jnp.sum(g * y, axis=2) @ w_out


def context_mixer(u, w_in, pool_w, pool_scale, q_norm, kv_norm, w_uq, w_ukv, w_branch, w_out):
    b, n, _ = u.shape
    na_q, na_k, na_v, pool_in, cq, ckv_raw, krope, gates = split_cols(u @ w_in)
    na_q = na_q.reshape(b, n, NA_HEADS, NA_HEAD_DIM) * (NA_HEAD_DIM ** -0.5)
    na_k = na_k.reshape(b, n, NA_HEADS, NA_HEAD_DIM)
    na_v = na_v.reshape(b, n, NA_HEADS, NA_HEAD_DIM)
    na_o = blocked_attention(na_q, na_k, na_v)
    pool_o = pool_mix(pool_in, pool_w, pool_scale)
    ckv = rmsnorm(ckv_raw, kv_norm)
    q_nope, q_rope = mla_queries(cq, q_norm, w_uq)
    k_nope, mv = mla_kv(ckv, w_ukv)
    q = jnp.concatenate([q_nope, q_rope], axis=-1) * ((MLA_NOPE + MLA_ROPE) ** -0.5)
    mla_o = blocked_attention(q, mla_keys(k_nope, krope), mv)
    out = merge_branches(na_o, pool_o, mla_o, gates, w_branch, w_out)
    return out, (na_k, na_v, ckv, krope)


def latent_mixer(u, na_kc, na_vc, ckv_c, krope_c, w_in, na_rpb, pool_w, pool_scale, q_norm, kv_norm, w_uq, w_ukv, w_branch, w_out):
    b, n, _ = u.shape
    na_q, na_k, na_v, pool_in, cq, ckv_raw, krope, gates = split_cols(u @ w_in)
    na_q = na_q.reshape(b, n, NA_HEADS, NA_HEAD_DIM) * (NA_HEAD_DIM ** -0.5)
    na_k = na_k.reshape(b, n, NA_HEADS, NA_HEAD_DIM)
    na_v = na_v.reshape(b, n, NA_HEADS, NA_HEAD_DIM)
    na_o = neighborhood_attention(na_q, na_k, na_v, na_kc, na_vc, na_rpb)
    pool_o = pool_mix(pool_in, pool_w, pool_scale)
    q_nope, q_rope = mla_queries(cq, q_norm, w_uq)
    q = jnp.concatenate([q_nope, axial_rope(q_rope, n)], axis=-1) * ((MLA_NOPE + MLA_ROPE) ** -0.5)
    ckv_all = jnp.concatenate([rmsnorm(ckv_raw, kv_norm), ckv_c], axis=1)
    krope_all = jnp.concatenate([axial_rope(krope, n), krope_c], axis=1)
    k_nope, mv = mla_kv(ckv_all, w_ukv)
    mla_o = blocked_attention(q, mla_keys(k_nope, krope_all), mv)
    return merge_branches(na_o, pool_o, mla_o, gates, w_branch, w_out)


def trunk_layer(x, mods, mixer_fn, norm_pre, norm_post, ffn_w_in, ffn_w_out):
    def pre(h, j):
        return rmsnorm(h, norm_pre[j]) * (1 + mods[:, :, 3 * j + 1]) + mods[:, :, 3 * j]

    def post(y, j):
        return mods[:, :, 3 * j + 2] * rmsnorm(y, norm_post[j])

    x = x + 0.5 * post(swiglu(pre(x, 0), ffn_w_in[0], ffn_w_out[0]), 0)
    y, aux = mixer_fn(pre(x, 1))
    x = x + post(y, 1)
    x = x + 0.5 * post(swiglu(pre(x, 2), ffn_w_in[1], ffn_w_out[1]), 2)
    return x, aux


def setup_inputs(seed: int = 0) -> dict:
    key = jax.random.key(seed)
    ks = jax.random.split(key, 24)

    def nrm(k, shape, scale=1.0):
        return scale * jax.random.normal(k, shape, jnp.float32)

    D = D_MODEL
    return {
        'x_prompt': nrm(ks[0], (BATCH, SEQ, D)),
        'x_sample': nrm(ks[1], (DEC_BATCH, DEC_SEQ, D)),
        'cache_na_k': nrm(ks[2], (DEC_BATCH, DEPTH, PAST_LEN, NA_HEADS, NA_HEAD_DIM)),
        'cache_na_v': nrm(ks[3], (DEC_BATCH, DEPTH, PAST_LEN, NA_HEADS, NA_HEAD_DIM)),
        'cache_mla_ckv': nrm(ks[4], (DEC_BATCH, DEPTH, PAST_LEN, MLA_KV_LORA)),
        'cache_mla_krope': nrm(ks[5], (DEC_BATCH, DEPTH, PAST_LEN, MLA_ROPE)),
        'c': nrm(ks[6], (DEC_BATCH, D)),
        'c_ctx': nrm(ks[7], (D,)),
        'w_ada': nrm(ks[8], (DEPTH, D, N_MOD * D), 0.5 * D ** -0.5),
        'b_ada': nrm(ks[9], (DEPTH, N_MOD * D), 0.02),
        'norm_pre': 1.0 + nrm(ks[10], (DEPTH, 3, D), 0.1),
        'norm_post': 1.0 + nrm(ks[11], (DEPTH, 3, D), 0.1),
        'ffn_w_in': nrm(ks[12], (DEPTH, 2, D, 2 * FFN_DIM), D ** -0.5),
        'ffn_w_out': nrm(ks[13], (DEPTH, 2, FFN_DIM, D), FFN_DIM ** -0.5),
        'w_in': nrm(ks[14], (DEPTH, D, IN_COLS), D ** -0.5),
        'na_rpb': nrm(ks[15], (DEPTH, NA_HEADS, 2 * NA_WIN_H - 1, 2 * NA_WIN_W - 1), 0.5),
        'pool_w': nrm(ks[16], (DEPTH, POOL_GROUPS, POOL_GROUP_DIM, POOL_GROUP_DIM), POOL_GROUP_DIM ** -0.5),
        'pool_scale': 1.0 + nrm(ks[17], (DEPTH, POOL_DIM), 0.1),
        'mla_q_norm': 1.0 + nrm(ks[18], (DEPTH, MLA_Q_LORA), 0.1),
        'mla_kv_norm': 1.0 + nrm(ks[19], (DEPTH, MLA_KV_LORA), 0.1),
        'mla_w_uq': nrm(ks[20], (DEPTH, MLA_Q_LORA, MLA_HEADS * (MLA_NOPE + MLA_ROPE)), MLA_Q_LORA ** -0.5),
        'mla_w_ukv': nrm(ks[21], (DEPTH, MLA_KV_LORA, MLA_HEADS * (MLA_NOPE + MLA_V)), MLA_KV_LORA ** -0.5),
        'w_branch': nrm(ks[22], (DEPTH, 3, BRANCH_W, D), BRANCH_W ** -0.5),
        'w_out': nrm(ks[23], (DEPTH, D, D), D ** -0.5),
    }


def reference(x_prompt, x_sample, cache_na_k, cache_na_v, cache_mla_ckv, cache_mla_krope, c, c_ctx,
              w_ada, b_ada, norm_pre, norm_post, ffn_w_in, ffn_w_out, w_in, na_rpb, pool_w, pool_scale,
              mla_q_norm, mla_kv_norm, mla_w_uq, mla_w_ukv, w_branch, w_out):
    xp, xs = x_prompt, x_sample
    st_k, st_v, st_ckv, st_kr = [], [], [], []
    for l in range(DEPTH):
        ffn_args = (norm_pre[l], norm_post[l], ffn_w_in[l], ffn_w_out[l])
        mods_ctx = modulations(c_ctx[None, :], w_ada[l], b_ada[l])
        mods_lat = modulations(c, w_ada[l], b_ada[l])
        xp, (k_l, v_l, ckv_l, kr_l) = trunk_layer(
            xp, mods_ctx,
            lambda u: context_mixer(u, w_in[l], pool_w[l], pool_scale[l], mla_q_norm[l], mla_kv_norm[l],
                                    mla_w_uq[l], mla_w_ukv[l], w_branch[l], w_out[l]),
            *ffn_args)
        xs, _ = trunk_layer(
            xs, mods_lat,
            lambda u: (latent_mixer(u, cache_na_k[:, l], cache_na_v[:, l], cache_mla_ckv[:, l], cache_mla_krope[:, l],
                                    w_in[l], na_rpb[l], pool_w[l], pool_scale[l], mla_q_norm[l], mla_kv_norm[l],
                                    mla_w_uq[l], mla_w_ukv[l], w_branch[l], w_out[l]), None),
            *ffn_args)
        st_k.append(k_l)
        st_v.append(v_l)
        st_ckv.append(ckv_l)
        st_kr.append(kr_l)
    state_na_k = jnp.stack(st_k, axis=1)
    state_na_v = jnp.stack(st_v, axis=1)
    state_mla_ckv = jnp.stack(st_ckv, axis=1)
    state_mla_krope = jnp.stack(st_kr, axis=1)
    return (xp, xs, state_na_k, state_na_v, state_mla_ckv, state_mla_krope)
```

```python
import math
from contextlib import ExitStack

import ml_dtypes
import numpy as np

import concourse.bass as bass
import concourse.mybir as mybir
from concourse.bass_utils import run_bass_kernel_spmd

F32 = mybir.dt.float32
BF16 = mybir.dt.bfloat16
AF = mybir.ActivationFunctionType
ALU = mybir.AluOpType

D = 1024
NS = 4096
NPR = 1024
NT = NS + NPR
KT = NS + 256 + NPR
KCTX = NS
KPR = NS + 256
TT = 256
NTILE = NT // TT
FF = 2816
NF = FF // 128
EPS = 1e-6
POOL_W = (2, 4, 8, 16)
NVEC = 272
NA_SCALE = 64 ** -0.5
MLA_SCALE = 96 ** -0.5


class Buf:
    __slots__ = ("name", "lw", "rd", "psum")

    def __init__(self, name, psum=False):
        self.name = name
        self.lw = None
        self.rd = []
        self.psum = psum


class Op:
    __slots__ = ("eng", "fn", "deps", "dma", "needed", "sem", "val")

    def __init__(self, eng, fn, dma):
        self.eng = eng
        self.fn = fn
        self.dma = dma
        self.deps = []
        self.needed = dma is not None
        self.sem = None
        self.val = 0


class Prog:
    ENGS = ("pe", "act", "dve", "pool", "sp")

    def __init__(self, nc, gstack):
        self.nc = nc
        self.gstack = gstack
        self.sems = {}
        self.counts = {}
        self.bufs = []
        self.nbar = 0
        self.nops = 0
        self._reset()

    def _reset(self):
        self.ops = {e: [] for e in self.ENGS}
        self.grp_last = {}
        for b in self.bufs:
            b.lw = None
            b.rd = []

    def buf(self, name, psum=False):
        b = Buf(name, psum)
        self.bufs.append(b)
        return b

    def get_sem(self, key):
        if key not in self.sems:
            self.sems[key] = self.gstack.enter_context(self.nc.semaphore("s%d" % len(self.sems)))
        return self.sems[key]

    def add(self, eng, fn, reads=(), writes=(), dma=None):
        op = Op(eng, fn, dma)
        self.nops += 1
        deps = {}

        def dep(d, kind):
            if d is None or d is op:
                return
            if d.dma is None and dma is None and d.eng == eng and eng != "pool":
                if kind != "raw" or eng == "pe":
                    return
            deps[id(d)] = d

        for b in reads:
            dep(b.lw, "raw")
            if b.psum:
                for r in b.rd:
                    if r.eng != eng or r.dma is not None:
                        dep(r, "rar")
        for b in writes:
            dep(b.lw, "waw")
            for r in b.rd:
                dep(r, "war")
        if dma is not None:
            prev = self.grp_last.get(dma)
            if prev is not None:
                deps[id(prev)] = prev
            self.grp_last[dma] = op
        op.deps = list(deps.values())
        for d in op.deps:
            d.needed = True
        wset = set(id(b) for b in writes)
        for b in writes:
            b.lw = op
            b.rd = []
        for b in reads:
            if id(b) not in wset:
                b.rd.append(op)
        self.ops[eng].append(op)
        return op

    def emit(self):
        nc = self.nc
        import os
        if os.environ.get("KDEBUG"):
            print("phase", self.nbar, "sbuf remaining", nc.sbuf_bytes_remaining, "ops", {e: len(v) for e, v in self.ops.items()})
        final_deps = []
        for e in self.ENGS:
            last = None
            for o in self.ops[e]:
                if o.dma is None:
                    last = o
            if last is not None:
                last.needed = True
                final_deps.append(last)
        for g, o in self.grp_last.items():
            final_deps.append(o)
        for e in self.ENGS:
            for o in self.ops[e]:
                if o.dma is not None:
                    key = ("d", o.dma)
                    self.counts[key] = self.counts.get(key, 0) + 16
                    o.sem, o.val = key, self.counts[key]
                    self.get_sem(key)
                elif o.needed:
                    key = ("e", e)
                    self.counts[key] = self.counts.get(key, 0) + 1
                    o.sem, o.val = key, self.counts[key]
                    self.get_sem(key)
        bar = self.get_sem(("bar",))
        self.nbar += 1
        nbar = self.nbar
        sems = self.sems
        ops = self.ops

        def run(e, engobj):
            known = {}
            for o in ops[e]:
                need = {}
                for d in o.deps:
                    if d.val > need.get(d.sem, 0):
                        need[d.sem] = d.val
                for s, v in need.items():
                    if known.get(s, 0) >= v:
                        continue
                    engobj.wait_ge(sems[s], v)
                    known[s] = v
                ins = o.fn(engobj)
                if o.sem is not None:
                    ins.then_inc(sems[o.sem], 16 if o.dma is not None else 1)
            if e == "sp":
                need = {}
                for d in final_deps:
                    if d.val > need.get(d.sem, 0):
                        need[d.sem] = d.val
                for s, v in need.items():
                    if known.get(s, 0) >= v:
                        continue
                    engobj.wait_ge(sems[s], v)
                engobj.sem_inc(bar, 1)
            else:
                engobj.wait_ge(bar, nbar)

        with nc.Block() as block:
            @block.tensor
            def _(pe):
                run("pe", pe)

            @block.scalar
            def _(act):
                run("act", act)

            @block.vector
            def _(dve):
                run("dve", dve)

            @block.gpsimd
            def _(pool):
                run("pool", pool)

            @block.sync
            def _(sp):
                run("sp", sp)
        self._reset()


ROPE_PERM = np.array(list(range(8, 16)) + list(range(0, 8)) + list(range(24, 32)) + list(range(16, 24)))


def _cs_table():
    cs = np.ones((128, NT), np.float64)
    t = np.arange(NS)
    row = (t // 64).astype(np.float64)
    col = (t % 64).astype(np.float64)
    inv = 10000.0 ** (-np.arange(8, dtype=np.float64) / 8)
    for d in range(32):
        pos = row if d < 16 else col
        ang = (pos.astype(np.float32) * inv[d % 8].astype(np.float32)).astype(np.float32)
        cs[64 + d, :NS] = np.cos(ang)
        sgn = -1.0 if (d % 16) < 8 else 1.0
        cs[96 + d, :NS] = sgn * np.sin(ang)
    cs[64:96, NS:] = 1.0
    cs[96:128, NS:] = 0.0
    return cs.astype(np.float32)


def _fold():
    f = np.zeros((128, 128), np.float32)
    for k in range(64, 128):
        for m in range(64, 128):
            if (k - 64) % 32 == (m - 64) % 32:
                f[k, m] = 1.0
    return f.astype(ml_dtypes.bfloat16)


def _bands():
    b = np.zeros((128, 4, 5, 128), np.float64)
    tp = np.arange(128)[:, None]
    t = np.arange(128)[None, :]
    for g, w in enumerate(POOL_W):
        hw = w // 2
        b[:, g, 0, :] = np.where(tp >= 128 + t - hw, 1.0 / w, 0.0)
        b[:, g, 1, :] = np.where(tp + 128 < t + hw, 1.0 / w, 0.0)
        inwin = (tp >= t - hw) & (tp < t + hw)
        eye = (tp == t).astype(np.float64)
        b[:, g, 2, :] = np.where(inwin, 1.0 / w, 0.0) - eye
        lo = np.maximum(t - hw, 0)
        cnt = (t + hw) - lo
        b[:, g, 3, :] = np.where(inwin, 1.0 / cnt, 0.0) - eye
        hi = np.minimum(t + hw, 128)
        cnt = hi - (t - hw)
        b[:, g, 4, :] = np.where(inwin, 1.0 / cnt, 0.0) - eye
    return b.astype(np.float32).astype(ml_dtypes.bfloat16)


def _rpb_tables(na_rpb):
    L = na_rpb.shape[0]
    p = np.arange(128)
    a = (p // 64)[:, None, None]
    kc = (p % 64)[:, None, None]
    jj = np.arange(24)[None, :, None]
    qc = np.arange(64)[None, None, :]
    relr = a + 18 - jj + 0 * qc + 0 * kc
    ws = np.clip(qc - 8, 0, 48)
    inwin = (kc >= ws) & (kc < ws + 16)
    cidx = np.clip(kc - qc + 15, 0, 30) + 0 * jj
    out = np.full((L, 128, 8, 2, 24, 64), -1e30, np.float32)
    for kind, (lo, hi) in enumerate(((0, 14), (3, 10))):
        ok = (relr >= lo) & (relr <= hi) & inwin
        rr = np.clip(relr, 0, 14)
        for l in range(L):
            for h in range(8):
                vals = na_rpb[l, h][rr, cidx]
                out[l, :, h, kind] = np.where(ok, vals, np.float32(-1e30))
    return out


def _na_plan():
    plan = []
    for t in range(8):
        ks = min(max(8 * t - 4, 0), 48)
        chunks = []
        for c in range(8):
            kr0 = ks + 2 * c
            Dd = kr0 - 8 * t
            rows = []
            for b in range(8):
                qr = 8 * t + b
                if qr < 4:
                    ok, kind = (kr0 <= 6), 0
                elif qr > 60:
                    ok, kind = (kr0 >= 56), 0
                else:
                    rs = qr - 4
                    ok = any(rs <= kr < rs + 8 for kr in (kr0, kr0 + 1))
                    kind = 1
                rows.append((ok, kind))
            valid = [b for b in range(8) if rows[b][0]]
            if not valid:
                chunks.append(None)
                continue
            b0, b1 = valid[0], valid[-1] + 1
            assert all(rows[b][0] for b in range(b0, b1))
            segs = []
            b = b0
            while b < b1:
                e = b
                while e < b1 and rows[e][1] == rows[b][1]:
                    e += 1
                jj0 = b - Dd + 11
                assert 0 <= jj0 and jj0 + (e - b) <= 24
                segs.append((b, e, rows[b][1], jj0))
                b = e
            chunks.append((b0, b1, segs))
        plan.append((ks, chunks))
    return plan


class Builder:
    def __init__(self):
        self.nc = bass.Bass("TRN2", target_bir_lowering=False)
        self.gst = ExitStack()

    def din(self, name, shape, dt=F32):
        return self.nc.dram_tensor(name, list(shape), dt, kind="ExternalInput").ap()

    def dout(self, name, shape, dt=F32):
        return self.nc.dram_tensor(name, list(shape), dt, kind="ExternalOutput").ap()

    def dscr(self, name, shape, dt=BF16):
        return self.nc.dram_tensor(name, list(shape), dt, kind="Internal").ap()

    def build(self):
        nc = self.nc
        with self.gst:
            self._build()
        return nc

    def sb(self, st, name, shape, dt):
        self.uid = getattr(self, "uid", 0) + 1
        return st.enter_context(self.nc.sbuf_tensor("sb%d_%s" % (self.uid, name), list(shape), dt))

    def dma(self, eng, out, in_, reads, writes, grp):
        if grp in ("w", "wa"):
            self.wrot = getattr(self, "wrot", 0) + 1
            grp = "w%d" % (self.wrot % 3)
        self.P.add(eng, lambda e: e.dma_start(out=out, in_=in_), reads=reads, writes=writes, dma=grp)

    def mm(self, out, lhsT, rhs, start, stop, reads, writes):
        self.P.add("pe", lambda e: e.matmul(out, lhsT=lhsT, rhs=rhs, start=start, stop=stop),
                   reads=reads, writes=writes)

    def act(self, out, in_, func, reads, writes, scale=None, bias=None):
        kw = {}
        if scale is not None:
            kw["scale"] = scale
        if bias is not None:
            kw["bias"] = bias
        self.P.add("act", lambda e: e.activation(out=out, in_=in_, func=func, **kw), reads=reads, writes=writes)

    def tt(self, eng, out, in0, in1, op, reads, writes):
        self.P.add(eng, lambda e: e.tensor_tensor(out=out, in0=in0, in1=in1, op=op), reads=reads, writes=writes)

    def ts(self, eng, out, in0, s1, s2, op0, op1, reads, writes):
        self.P.add(eng, lambda e: e.tensor_scalar(out=out, in0=in0, scalar1=s1, scalar2=s2, op0=op0, op1=op1),
                   reads=reads, writes=writes)

    def stt(self, eng, out, in0, scalar, in1, op0, op1, reads, writes):
        self.P.add(eng, lambda e: e.scalar_tensor_tensor(out=out, in0=in0, scalar=scalar, in1=in1, op0=op0, op1=op1),
                   reads=reads, writes=writes)

    def cp(self, eng, out, in_, reads, writes):
        if eng == "act":
            self.P.add("act", lambda e: e.copy(out=out, in_=in_), reads=reads, writes=writes)
        else:
            self.P.add(eng, lambda e: e.tensor_copy(out=out, in_=in_), reads=reads, writes=writes)

    def rstd_from(self, out, ps_ap, reads, writes):
        self.act(out, ps_ap, AF.Sqrt, reads, writes, scale=1.0, bias=EPS)
        self.P.add("dve", lambda e: e.reciprocal(out=out, in_=out), reads=writes, writes=writes)

    def _build(self):
        nc = self.nc
        gst = self.gst
        P = self.P = Prog(nc, gst)
        d = self.d = {}
        d["xT"] = self.din("xT", [D, NT])
        d["vecs"] = self.din("vecs", [128, NVEC])
        d["cs"] = self.din("cs", [128, NT])
        d["w_ada"] = self.din("w_ada", [2, D, 9 * D])
        d["ffn_w_in"] = self.din("ffn_w_in", [2, 2, D, 2 * FF])
        d["ffn_w_out"] = self.din("ffn_w_out", [2, 2, FF, D])
        d["w_in"] = self.din("w_in", [2, D, 5664])
        d["w_kr2"] = self.din("w_kr2", [2, D, 128])
        d["w_uq2"] = self.din("w_uq2", [2, 256, 1024])
        d["w_ukv_k"] = self.din("w_ukv_k", [2, 256, 512])
        d["w_ukv_v"] = self.din("w_ukv_v", [2, 256, 512])
        d["pool_w"] = self.din("pool_w", [2, 4, 128, 128])
        d["w_branch"] = self.din("w_branch", [2, 3, 512, D])
        d["w_out"] = self.din("w_out", [2, D, D])
        d["badd"] = self.din("badd", [2, 128, 16, 1536])
        d["bands"] = self.din("bands", [128, 20 * 128], BF16)
        d["fold"] = self.din("fold", [128, 128], BF16)
        d["ident"] = self.din("ident", [128, 128], BF16)
        d["c_kT"] = self.din("c_kT", [2, 512, 256])
        d["c_v"] = self.din("c_v", [2, 256, 512])
        d["c_ckvT"] = self.din("c_ckvT", [2, 256, 256])
        d["c_krT"] = self.din("c_krT", [2, 32, 256])
        d["yT"] = self.dout("yT", [D, NT])
        d["st_k"] = self.dout("st_k", [2, 512, NPR])
        d["st_v"] = self.dout("st_v", [2, NPR, 512])
        d["st_ckv"] = self.dout("st_ckv", [2, 256, NPR])
        d["st_kr"] = self.dout("st_kr", [2, 32, NPR])
        d["U"] = self.dscr("U", [D, NT])
        d["NAQ"] = self.dscr("NAQ", [512, NT])
        d["NAK"] = self.dscr("NAK", [512, KT])
        d["NAV"] = self.dscr("NAV", [KT, 1024])
        d["KN"] = self.dscr("KN", [512, KT])
        d["KR"] = self.dscr("KR", [64, KT])
        d["MV"] = self.dscr("MV", [KT, 1024])
        d["MQ"] = self.dscr("MQ", [8, 128, NT])
        d["PIN"] = self.dscr("PIN", [NT, 512])
        d["NAO"] = self.dscr("NAO", [512, NT])
        d["MO"] = self.dscr("MO", [512, NT])
        self.bd = {k: P.buf("d_" + k) for k in ("X", "U", "NAQ", "NAK", "NAV", "KN", "KR", "MV", "MQ", "PIN", "NAO", "MO", "OUT")}

        self.vecs = self.sb(gst, "vecs", [128, NVEC], F32)
        self.mods = self.sb(gst, "mods", [128, 2, 72, 2], F32)
        self.AG = self.sb(gst, "AG", [128, 2, 3, 2, 2, 8], F32)
        self.onesD = self.sb(gst, "onesD", [128, 128], BF16)
        self.ones256 = self.sb(gst, "ones256", [128, 128], BF16)
        self.b_const = P.buf("const")
        self.psw = [gst.enter_context(nc.psum_tensor("psw%d" % i, [128, 2, 512], F32)) for i in range(4)]
        self.bps = [P.buf("ps%d" % i, psum=True) for i in range(8)]

        self.phase_setup()
        for l in range(2):
            self.phase_ffn(l, 0)
            with ExitStack() as sE:
                E = self.sb(sE, "E", [128, 16, 1536], BF16)
                self.phase_inproj(l, E)
                self.phase_na_sample(l, E)
            self.phase_mla_sample(l, None)
            with ExitStack() as sW:
                MW = {"wg": self.sb(sW, "wg", [128, 8, 3072], BF16), "wbr": self.sb(sW, "wbr", [128, 12, D], BF16),
                      "wo": self.sb(sW, "wo", [128, 8, D], BF16), "pw": self.sb(sW, "pw", [128, 4, 128], BF16),
                      "bands": self.sb(sW, "bands", [128, 4, 5, 128], BF16)}
                self.phase_merge(l, MW, load_w=True)
            self.phase_ffn(l, 1)

    def ps(self, i):
        return self.psw[i // 2][:, i % 2, :], self.bps[i]

    def phase_setup(self):
        P, d = self.P, self.d
        with ExitStack() as st:
            sc = self.sb(st, "sc", [128, 8, 2], BF16)
            wa = [self.sb(st, "wa%d" % i, [128, 8, 2304], BF16) for i in range(2)]
            b_sc = P.buf("sc")
            b_wa = [[P.buf("wa%d_%d" % (i, k)) for k in range(8)] for i in range(2)]
            b_mods = P.buf("mods")
            bc = self.b_const
            self.dma("sp", self.vecs[:], d["vecs"], [], [bc], "c0")
            P.add("dve", lambda e: e.memset(self.onesD[:], 1.0 / 1024), writes=[bc])
            P.add("dve", lambda e: e.memset(self.ones256[:], 1.0 / 256), writes=[bc])
            self.act(sc[:].rearrange("p k v -> p (k v)"), self.vecs[:, 0:16], AF.Silu, [bc], [b_sc])
            psm, bpsm = self.ps(0)
            for l in range(2):
                for pc in range(4):
                    wi = (l * 4 + pc) % 2
                    for kc in range(8):
                        self.dma("pool", wa[wi][:, kc, :], d["w_ada"][l, kc * 128:(kc + 1) * 128, pc * 2304:(pc + 1) * 2304],
                                 [], [b_wa[wi][kc]], "wa")
                    for mm_ in range(18):
                        m = pc * 18 + mm_
                        for kc in range(8):
                            self.mm(psm[:, 2 * m:2 * m + 2], wa[wi][:, kc, mm_ * 128:(mm_ + 1) * 128], sc[:, kc, :],
                                    kc == 0, kc == 7, [b_wa[wi][kc], b_sc], [bpsm])
                psv = psm[:, 0:144].rearrange("p (m v) -> p m v", v=2)
                for v in range(2):
                    self.tt("dve", self.mods[:, l, :, v], psv[:, :, v], self.vecs[:, 16 + l * 72:16 + (l + 1) * 72], ALU.add,
                            [bpsm, bc], [b_mods])
                for j in range(3):
                    cj = 1.0 if j == 1 else 0.5
                    for v in range(2):
                        npre = self.vecs[:, 160 + l * 24 + j * 8:160 + l * 24 + j * 8 + 8]
                        npost = self.vecs[:, 208 + l * 24 + j * 8:208 + l * 24 + j * 8 + 8]
                        self.stt("dve", self.AG[:, l, j, v, 0, :], self.mods[:, l, (3 * j + 1) * 8:(3 * j + 2) * 8, v], 1.0, npre,
                                 ALU.add, ALU.mult, [b_mods, bc], [bc])
                        self.stt("dve", self.AG[:, l, j, v, 1, :], self.mods[:, l, (3 * j + 2) * 8:(3 * j + 3) * 8, v], cj, npost,
                                 ALU.mult, ALU.mult, [b_mods, bc], [bc])
            P.emit()

    def mod_A(self, l, j, v, kc):
        return self.AG[:, l, j, v, 0, kc:kc + 1]

    def mod_G(self, l, j, v, kc):
        return self.AG[:, l, j, v, 1, kc:kc + 1]

    def mod_B(self, l, j, v, kc):
        return self.mods[:, l, 3 * j * 8 + kc, v:v + 1]

    def xsrc(self, l, which):
        return self.d["xT"] if (l == 0 and which == 0) else self.d["yT"]

    def prenorm(self, l, j, v, xb, b_xb, sq, b_sq, rstd, b_rstd, t2, b_t2, u, b_u, stat_bank, T=TT, square=True, rest=True):
        bc = self.b_const
        if square:
            self.act(sq[:], xb[:], AF.Square, [b_xb], [b_sq])
        if not rest:
            return
        pst, bpst = self.ps(stat_bank)
        for kc in range(8):
            self.mm(pst[:, 0:T], self.onesD[:], sq[:, kc, :], kc == 0, kc == 7, [bc, b_sq], [bpst])
        self.rstd_from(rstd[:], pst[:, 0:T], [bpst], [b_rstd])
        self.tt("dve", t2[:], xb[:], rstd[:].unsqueeze(1).to_broadcast([128, 8, T]), ALU.mult, [b_xb, b_rstd], [b_t2])
        for kc in range(8):
            self.act(u[:, kc, :], t2[:, kc, :], AF.Identity, [b_t2, bc], [b_u],
                     scale=self.mod_A(l, j, v, kc), bias=self.mod_B(l, j, v, kc))

    def postnorm_residual(self, l, j, v, ys, b_ys, ysq, b_ysq, rstd, b_rstd, xb, b_xb, stat_bank, T=TT):
        bc = self.b_const
        pst, bpst = self.ps(stat_bank)
        for kc in range(8):
            self.mm(pst[:, 0:T], self.onesD[:], ysq[:, kc, :], kc == 0, kc == 7, [bc, b_ysq], [bpst])
        self.rstd_from(rstd[:], pst[:, 0:T], [bpst], [b_rstd])
        self.tt("dve", ys[:], ys[:], rstd[:].unsqueeze(1).to_broadcast([128, 8, T]), ALU.mult, [b_ys, b_rstd], [b_ys])
        self.tt("dve", xb[:], xb[:], ys[:], ALU.add, [b_xb, b_ys], [b_xb])

    def phase_ffn(self, l, which):
        P, d = self.P, self.d
        j = 0 if which == 0 else 2
        xsrc = self.xsrc(l, which)
        xdst = d["yT"]
        bX = self.bd["X"]
        with ExitStack() as st:
            win = self.sb(st, "win", [128, 8, 2 * FF], BF16)
            wout = self.sb(st, "wout", [128, NF, D], BF16)
            xt = [self.sb(st, "xt%d" % i, [128, 8, TT], F32) for i in range(3)]
            u = self.sb(st, "u", [128, 8, TT], BF16)
            sq = self.sb(st, "sq", [128, 8, TT], BF16)
            ysq = self.sb(st, "ysq", [128, 8, TT], BF16)
            t2 = self.sb(st, "t2", [128, 8, TT], F32)
            ys = self.sb(st, "ys", [128, 8, TT], F32)
            g = self.sb(st, "g", [128, NF, TT], BF16)
            sg = [self.sb(st, "sg%d" % i, [128, TT], F32) for i in range(2)]
            rs1 = self.sb(st, "rs1", [128, TT], F32)
            rs2 = self.sb(st, "rs2", [128, TT], F32)
            FG = [0, 6, 12, 17, 22]
            b_win = [[[P.buf("win%d_%d_%d" % (gi, k, hf)) for hf in range(2)] for k in range(8)] for gi in range(4)]

            def fgrp(f):
                for gi in range(4):
                    if FG[gi] <= f < FG[gi + 1]:
                        return gi
            b_wout = [[P.buf("wout%d_%d" % (k, ch)) for ch in range(2)] for k in range(4)]
            b_xt = [P.buf("xt0"), P.buf("xt1"), P.buf("xt2")]
            b_u, b_sq, b_t2, b_ys = [P.buf(n) for n in ("u", "sq", "t2", "ys")]
            b_ysq = [P.buf("ysq%d" % k) for k in range(8)]
            b_g = [P.buf("g%d" % f) for f in range(NF)]
            b_sg = [P.buf("sg0"), P.buf("sg1")]
            b_rs1, b_rs2 = P.buf("rs1"), P.buf("rs2")
            bc = self.b_const
            for gi in range(4):
                for kc in range(8):
                    for hf in range(2):
                        c0, c1 = hf * FF + FG[gi] * 128, hf * FF + FG[gi + 1] * 128
                        self.dma("pool", win[:, kc, c0:c1], d["ffn_w_in"][l, which, kc * 128:(kc + 1) * 128, c0:c1],
                                 [], [b_win[gi][kc][hf]], "w")
            wo_v = d["ffn_w_out"][l, which].rearrange("(f p) n -> p f n", p=128)
            fsp = [0, 6, 12, 17, 22]
            for ch in range(2):
                for q in range(4):
                    self.dma("pool", wout[:, fsp[q]:fsp[q + 1], ch * 512:(ch + 1) * 512],
                             wo_v[:, fsp[q]:fsp[q + 1], ch * 512:(ch + 1) * 512], [], [b_wout[q][ch]], "w")

            def wout_buf(f, dc):
                for q in range(4):
                    if fsp[q] <= f < fsp[q + 1]:
                        return b_wout[q][dc // 4]

            def load(i):
                self.dma("sp", xt[i % 3][:], xsrc[:, i * TT:(i + 1) * TT].rearrange("(k p) t -> p k t", p=128),
                         [bX], [b_xt[i % 3]], "xl%d" % (i % 3))

            def sqpart(i):
                v = 0 if i < 16 else 1
                self.prenorm(l, j, v, xt[i % 3], b_xt[i % 3], sq, b_sq, rs1, b_rs1, t2, b_t2, u, b_u, 6, rest=False)

            def pro_b(i):
                v = 0 if i < 16 else 1
                self.prenorm(l, j, v, xt[i % 3], b_xt[i % 3], sq, b_sq, rs1, b_rs1, t2, b_t2, u, b_u, 6, square=False)

            def hpart(i, hook):
                for f in range(NF):
                    bank = f % 4
                    pb, bpb = self.ps(bank)
                    for half in range(2):
                        for kc in range(8):
                            self.mm(pb[:, half * TT:(half + 1) * TT], win[:, kc, half * FF + f * 128:half * FF + (f + 1) * 128],
                                    u[:, kc, :], kc == 0, kc == 7, [b_win[fgrp(f)][kc][half], b_u], [bpb])
                    self.act(sg[f % 2][:], pb[:, 0:TT], AF.Silu, [bpb], [b_sg[f % 2]])
                    self.tt("dve", g[:, f, :], sg[f % 2][:], pb[:, TT:2 * TT], ALU.mult, [b_sg[f % 2], bpb], [b_g[f]])
                    if f == 1 and hook is not None:
                        hook()

            pst2, bpst2 = self.ps(7)

            def ypart(i):
                v = 0 if i < 16 else 1
                for dc in range(8):
                    pb, bpb = self.ps(4 + dc % 2)
                    for f in range(NF):
                        self.mm(pb[:, 0:TT], wout[:, f, dc * 128:(dc + 1) * 128], g[:, f, :], f == 0, f == NF - 1,
                                [wout_buf(f, dc), b_g[f]], [bpb])
                    if dc >= 1:
                        self.mm(pst2[:, 0:TT], self.onesD[:], ysq[:, dc - 1, :], dc == 1, False, [bc, b_ysq[dc - 1]], [bpst2])
                    self.act(ysq[:, dc, :], pb[:, 0:TT], AF.Square, [bpb], [b_ysq[dc]])
                    self.act(ys[:, dc, :], pb[:, 0:TT], AF.Identity, [bpb, bc], [b_ys], scale=self.mod_G(l, j, v, dc))

            def epilogue(i):
                xb, b_xb = xt[i % 3], b_xt[i % 3]
                self.mm(pst2[:, 0:TT], self.onesD[:], ysq[:, 7, :], False, True, [bc, b_ysq[7]], [bpst2])
                self.rstd_from(rs2[:], pst2[:, 0:TT], [bpst2], [b_rs2])
                self.tt("dve", ys[:], ys[:], rs2[:].unsqueeze(1).to_broadcast([128, 8, TT]), ALU.mult, [b_ys, b_rs2], [b_ys])
                self.tt("dve", xb[:], xb[:], ys[:], ALU.add, [b_xb, b_ys], [b_xb])
                self.dma("pool", xdst[:, i * TT:(i + 1) * TT].rearrange("(k p) t -> p k t", p=128), xb[:],
                         [b_xb], [bX], "xs%d" % (i % 3))

            load(0)
            load(1)
            sqpart(0)
            pro_b(0)
            for i in range(NTILE):
                if i + 1 < NTILE:
                    sqpart(i + 1)

                def hook(i=i):
                    if i >= 1:
                        epilogue(i - 1)
                    if i + 2 < NTILE:
                        load(i + 2)

                hpart(i, hook)
                if i + 1 < NTILE:
                    pro_b(i + 1)
                ypart(i)
            epilogue(NTILE - 1)
            P.emit()

    def phase_inproj(self, l, E):
        P, d = self.P, self.d
        bd = self.bd
        bc = self.b_const
        with ExitStack() as st:
            wqk = self.sb(st, "wqk", [128, 8, 1024], BF16)
            wvp = self.sb(st, "wvp", [128, 8, 1024], BF16)
            wc = self.sb(st, "wc", [128, 8, 512], BF16)
            wkr = self.sb(st, "wkr", [128, 8, 128], BF16)
            wuq = self.sb(st, "wuq", [128, 2, 1024], BF16)
            wuk = self.sb(st, "wuk", [128, 2, 512], BF16)
            wuv = self.sb(st, "wuv", [128, 2, 512], BF16)
            fold = self.sb(st, "fold", [128, 128], BF16)
            xt = [self.sb(st, "xt%d" % i, [128, 8, TT], F32) for i in range(2)]
            cst = [self.sb(st, "cst%d" % i, [128, TT], F32) for i in range(2)]
            u = self.sb(st, "u", [128, 8, TT], BF16)
            sq = self.sb(st, "sq", [128, 8, TT], BF16)
            t2 = self.sb(st, "t2", [128, 8, TT], F32)
            rs1 = self.sb(st, "rs1", [128, TT], F32)
            qko = self.sb(st, "qko", [128, 8, TT], BF16)
            kst = self.sb(st, "kst", [128, 4, TT], F32)
            csb = self.sb(st, "csb", [128, 2, 2, TT], F32)
            csq = self.sb(st, "csq", [128, 2, 2, TT], BF16)
            rsc = self.sb(st, "rsc", [128, 2, TT], F32)
            cqn = self.sb(st, "cqn", [128, 2, TT], BF16)
            ckf = self.sb(st, "ckf", [128, 2, TT], F32)
            ckn = self.sb(st, "ckn", [128, 2, TT], BF16)
            mq = self.sb(st, "mq", [128, 8, TT], BF16)
            kn = self.sb(st, "kn", [64, 8, TT], BF16)
            krst = self.sb(st, "krst", [128, TT], F32)
            krt = self.sb(st, "krt", [128, TT], BF16)
            krd = self.sb(st, "krd", [128, TT], BF16)
            vaug = self.sb(st, "vaug", [128, 2, 8, 128], BF16)
            vst = self.sb(st, "vst", [128, 2, 512], F32)
            pin = self.sb(st, "pin", [128, 2, 512], BF16)
            mvaug = self.sb(st, "mvaug", [128, 2, 8, 128], BF16)
            stg = self.sb(st, "stg", [128, 4, 256], F32)
            b_w = {k: P.buf("w_" + k) for k in ("qk", "vp", "c", "kr", "uq", "uk", "uv", "fold")}
            b_xt = [P.buf("xt0"), P.buf("xt1")]
            b_cst = [P.buf("cst0"), P.buf("cst1")]
            (b_u, b_sq, b_t2, b_rs1, b_qko, b_kst, b_csb_, b_csq_, b_rsc, b_cqn, b_ckf, b_ckn, b_mq, b_kn, b_krst, b_krt,
             b_krd, b_vaug, b_vst, b_pin, b_mvaug, b_stg) = [P.buf("b%d" % i) for i in range(22)]
            b_csb = [P.buf("csb0"), P.buf("csb1")]
            b_csq = [P.buf("csq0"), P.buf("csq1")]
            wv = d["w_in"][l].rearrange("(k p) n -> p k n", p=128)
            for k2 in range(2):
                ks = slice(4 * k2, 4 * k2 + 4)
                self.dma("pool", wqk[:, ks, :], wv[:, ks, 0:1024], [], [b_w["qk"]], "w")
                self.dma("pool", wvp[:, ks, :], wv[:, ks, 1024:2048], [], [b_w["vp"]], "w")
            self.dma("pool", wc[:], wv[:, :, 2048:2560], [], [b_w["c"]], "w")
            self.dma("pool", wkr[:], d["w_kr2"][l].rearrange("(k p) n -> p k n", p=128), [], [b_w["kr"]], "w")
            self.dma("pool", wuq[:], d["w_uq2"][l].rearrange("(k p) n -> p k n", p=128), [], [b_w["uq"]], "w")
            self.dma("pool", wuk[:], d["w_ukv_k"][l].rearrange("(k p) n -> p k n", p=128), [], [b_w["uk"]], "w")
            self.dma("pool", wuv[:], d["w_ukv_v"][l].rearrange("(k p) n -> p k n", p=128), [], [b_w["uv"]], "w")
            self.dma("sp", fold[:], d["fold"], [], [b_w["fold"]], "c0")
            P.add("dve", lambda e: e.memset(vaug[:], 1.0), writes=[b_vaug])
            P.add("dve", lambda e: e.memset(mvaug[:], 1.0), writes=[b_mvaug])
            for i4 in range(4):
                self.dma("pool", E[:, 4 * i4:4 * i4 + 4, :], d["badd"][l, :, 4 * i4:4 * i4 + 4, :], [], [P.buf("E%d" % i4)], "w")
            qn = lambda kc: self.vecs[:, 264 + l * 2 + kc:264 + l * 2 + kc + 1]
            kvn = lambda kc: self.vecs[:, 268 + l * 2 + kc:268 + l * 2 + kc + 1]
            rot = [0]

            def bank():
                rot[0] = (rot[0] + 1) % 6
                return self.ps(rot[0])

            def evac_eng():
                return "act" if rot[0] % 2 == 0 else "dve"

            def knope_and_mv(ckn_t, b_ckn_t, kt0):
                for hp in range(4):
                    pb, bpb = bank()
                    for hh in range(2):
                        h = 2 * hp + hh
                        for kc in range(2):
                            self.mm(pb[0:64, hh * TT:(hh + 1) * TT], wuk[:, kc, h * 64:(h + 1) * 64], ckn_t[:, kc, :],
                                    kc == 0, kc == 1, [b_w["uk"], b_ckn_t], [bpb])
                    self.cp(evac_eng(), kn[:, 2 * hp:2 * hp + 2, :], pb[0:64, :].rearrange("p (h t) -> p h t", h=2),
                            [bpb], [b_kn])
                self.dma("pool", d["KN"][:, kt0:kt0 + TT].rearrange("(h p) t -> p h t", p=64), kn[:], [b_kn], [bd["KN"]], "so1")
                for blk in range(2):
                    pb, bpb = bank()
                    for kc in range(2):
                        self.mm(pb[:, :], ckn_t[:, kc, blk * 128:(blk + 1) * 128], wuv[:, kc, :], kc == 0, kc == 1,
                                [b_w["uv"], b_ckn_t], [bpb])
                    self.cp(evac_eng(), mvaug[:, blk, :, 0:64], pb[:, :].rearrange("p (h e) -> p h e", h=8), [bpb], [b_mvaug])
                self.dma("pool", d["MV"][kt0:kt0 + TT, :].rearrange("(b p) f -> p b f", p=128),
                         mvaug[:].rearrange("p b h e -> p b (h e)"), [b_mvaug], [bd["MV"]], "so2")

            kt0 = KCTX
            self.dma("sp", stg[:], d["c_kT"][l].rearrange("(c p) t -> p c t", p=128), [], [b_stg], "cx")
            self.cp("dve", qko[:, 4:8, :], stg[:], [b_stg], [b_qko])
            self.dma("pool", d["NAK"][:, kt0:kt0 + TT].rearrange("(c p) t -> p c t", p=128), qko[:, 4:8, :], [b_qko], [bd["NAK"]], "so3")
            self.dma("sp", stg[:].rearrange("p (b x) t -> p b (x t)", b=2), d["c_v"][l].rearrange("(b p) f -> p b f", p=128),
                     [b_qko], [b_stg], "cx")
            self.cp("dve", vaug[:, :, :, 0:64], stg[:].rearrange("p (b x) t -> p b (x t)", b=2).rearrange("p b (h e) -> p b h e", h=8),
                    [b_stg], [b_vaug])
            self.dma("pool", d["NAV"][kt0:kt0 + TT, :].rearrange("(b p) f -> p b f", p=128),
                     vaug[:].rearrange("p b h e -> p b (h e)"), [b_vaug], [bd["NAV"]], "so4")
            self.dma("sp", stg[:, 0:2, :], d["c_ckvT"][l].rearrange("(c p) t -> p c t", p=128), [b_vaug], [b_stg], "cx")
            self.cp("dve", ckn[:], stg[:, 0:2, :], [b_stg], [b_ckn])
            knope_and_mv(ckn, b_ckn, kt0)
            self.dma("sp", stg[64:96, 2, :], d["c_krT"][l], [], [b_stg], "cx")
            self.dma("sp", stg[96:128, 2, :], d["c_krT"][l], [], [b_stg], "cx")
            self.cp("dve", krd[64:128, :], stg[64:128, 2, :], [b_stg], [b_krd])
            self.dma("pool", d["KR"][:, kt0:kt0 + TT], krd[64:128, :], [b_krd], [bd["KR"]], "so5")

            xsrc = d["yT"]

            def load(i):
                self.dma("sp", xt[i % 2][:], xsrc[:, i * TT:(i + 1) * TT].rearrange("(k p) t -> p k t", p=128),
                         [bd["X"]], [b_xt[i % 2]], "xl%d" % (i % 2))
                self.dma("sp", cst[i % 2][:], d["cs"][:, i * TT:(i + 1) * TT], [], [b_cst[i % 2]], "cl%d" % (i % 2))

            def tinfo(i):
                v = 0 if i < 16 else 1
                t0 = i * TT
                kt0 = t0 if i < 16 else KPR + (i - 16) * TT
                pt0 = (i - 16) * TT
                return v, t0, kt0, pt0

            def part_A(i):
                v, t0, kt0, pt0 = tinfo(i)
                self.prenorm(l, 1, v, xt[i % 2], b_xt[i % 2], sq, b_sq, rs1, b_rs1, t2, b_t2, u, b_u, 6)
                self.dma("pool", d["U"][:, t0:t0 + TT].rearrange("(k p) t -> p k t", p=128), u[:], [b_u], [bd["U"]], "so0")

            def part_B(i):
                v, t0, kt0, pt0 = tinfo(i)
                for cpair in range(4):
                    pb, bpb = bank()
                    for hh in range(2):
                        c = 2 * cpair + hh
                        for kc in range(8):
                            self.mm(pb[:, hh * TT:(hh + 1) * TT], wqk[:, kc, c * 128:(c + 1) * 128], u[:, kc, :], kc == 0, kc == 7,
                                    [b_w["qk"], b_u], [bpb])
                    pv = pb[:, :].rearrange("p (h t) -> p h t", h=2)
                    if cpair < 2:
                        self.act(qko[:, 2 * cpair:2 * cpair + 2, :], pv, AF.Copy, [bpb], [b_qko], scale=NA_SCALE)
                    else:
                        self.cp(evac_eng(), qko[:, 2 * cpair:2 * cpair + 2, :], pv, [bpb], [b_qko])
                    if v == 1 and cpair >= 2:
                        self.cp("act", kst[:, 2 * (cpair - 2):2 * (cpair - 2) + 2, :], pv, [bpb], [b_kst])
                self.dma("pool", d["NAQ"][:, t0:t0 + TT].rearrange("(c p) t -> p c t", p=128), qko[:, 0:4, :], [b_qko], [bd["NAQ"]], "so6")
                self.dma("pool", d["NAK"][:, kt0:kt0 + TT].rearrange("(c p) t -> p c t", p=128), qko[:, 4:8, :], [b_qko], [bd["NAK"]], "so3")
                if v == 1:
                    self.dma("pool", d["st_k"][l][:, pt0:pt0 + TT].rearrange("(c p) t -> p c t", p=128), kst[:], [b_kst], [bd["OUT"]], "so7")

            def part_C(i):
                v, t0, kt0, pt0 = tinfo(i)
                pbs = []
                for which in range(2):
                    pb, bpb = bank()
                    pbs.append((pb, bpb))
                    for hh in range(2):
                        c = 2 * which + hh
                        for kc in range(8):
                            self.mm(pb[:, hh * TT:(hh + 1) * TT], wc[:, kc, c * 128:(c + 1) * 128], u[:, kc, :], kc == 0, kc == 7,
                                    [b_w["c"], b_u], [bpb])
                    pv = pb[:, :].rearrange("p (h t) -> p h t", h=2)
                    self.cp("dve", csb[:, which, :, :], pv, [bpb], [b_csb[which]])
                    self.act(csq[:, which, :, :], pv, AF.Square, [bpb], [b_csq[which]])
                pst, bpst = self.ps(7)
                for which in range(2):
                    for kc in range(2):
                        self.mm(pst[:, which * TT:(which + 1) * TT], self.ones256[:], csq[:, which, kc, :], kc == 0, kc == 1,
                                [bc, b_csq[which]], [bpst])
                self.rstd_from(rsc[:].rearrange("p w t -> p (w t)"), pst[:, 0:2 * TT], [bpst], [b_rsc])
                for kc in range(2):
                    self.stt("dve", cqn[:, kc, :], csb[:, 0, kc, :], qn(kc), rsc[:, 0, :], ALU.mult, ALU.mult,
                             [b_csb[0], b_rsc, bc], [b_cqn])
                    self.stt("dve", ckf[:, kc, :], csb[:, 1, kc, :], kvn(kc), rsc[:, 1, :], ALU.mult, ALU.mult,
                             [b_csb[1], b_rsc, bc], [b_ckf])
                self.cp("act", ckn[:], ckf[:], [b_ckf], [b_ckn])
                if v == 1:
                    self.dma("pool", d["st_ckv"][l][:, pt0:pt0 + TT].rearrange("(c p) t -> p c t", p=128), ckf[:], [b_ckf], [bd["OUT"]], "so8")

            def part_D(i):
                v, t0, kt0, pt0 = tinfo(i)
                cs_t, b_cs = cst[i % 2], b_cst[i % 2]
                pb, bpb = bank()
                for kc in range(8):
                    self.mm(pb[:, 0:TT], wkr[:, kc, :], u[:, kc, :], kc == 0, kc == 7, [b_w["kr"], b_u], [bpb])
                if v == 1:
                    self.cp("act", krst[64:96, :], pb[64:96, 0:TT], [bpb], [b_krst])
                    self.dma("pool", d["st_kr"][l][:, pt0:pt0 + TT], krst[64:96, :], [b_krst], [bd["OUT"]], "so10")
                self.tt("dve", krt[:], pb[:, 0:TT], cs_t[:], ALU.mult, [bpb, b_cs], [b_krt])
                for blk in range(2):
                    pb, bpb = bank()
                    for kc in range(8):
                        self.mm(pb[:, :], u[:, kc, blk * 128:(blk + 1) * 128], wvp[:, kc, 0:512], kc == 0, kc == 7, [b_w["vp"], b_u], [bpb])
                    self.cp(evac_eng(), vaug[:, blk, :, 0:64], pb[:, :].rearrange("p (h e) -> p h e", h=8), [bpb], [b_vaug])
                    if v == 1:
                        self.cp("act", vst[:, blk, :], pb[:, :], [bpb], [b_vst])
                    pb, bpb = bank()
                    for kc in range(8):
                        self.mm(pb[:, :], u[:, kc, blk * 128:(blk + 1) * 128], wvp[:, kc, 512:1024], kc == 0, kc == 7, [b_w["vp"], b_u], [bpb])
                    self.cp(evac_eng(), pin[:, blk, :], pb[:, :], [bpb], [b_pin])
                self.dma("pool", d["NAV"][kt0:kt0 + TT, :].rearrange("(b p) f -> p b f", p=128),
                         vaug[:].rearrange("p b h e -> p b (h e)"), [b_vaug], [bd["NAV"]], "so4")
                self.dma("pool", d["PIN"][t0:t0 + TT, :].rearrange("(b p) f -> p b f", p=128), pin[:], [b_pin], [bd["PIN"]], "so11")
                if v == 1:
                    self.dma("pool", d["st_v"][l][pt0:pt0 + TT, :].rearrange("(b p) f -> p b f", p=128), vst[:], [b_vst], [bd["OUT"]], "so12")
                pb2, bpb2 = bank()
                self.mm(pb2[:, 0:TT], fold[:], krt[:], True, True, [b_w["fold"], b_krt], [bpb2])
                self.cp("act", krd[64:128, :], pb2[64:128, 0:TT], [bpb2], [b_krd])
                self.dma("pool", d["KR"][:, kt0:kt0 + TT], krd[64:128, :], [b_krd], [bd["KR"]], "so5")

            def part_E(i):
                v, t0, kt0, pt0 = tinfo(i)
                cs_t, b_cs = cst[i % 2], b_cst[i % 2]
                for hp in range(4):
                    pb, bpb = bank()
                    for hh in range(2):
                        h = 2 * hp + hh
                        for kc in range(2):
                            self.mm(pb[:, hh * TT:(hh + 1) * TT], wuq[:, kc, h * 128:(h + 1) * 128], cqn[:, kc, :], kc == 0, kc == 1,
                                    [b_w["uq"], b_cqn], [bpb])
                    self.tt("dve", mq[:, 2 * hp:2 * hp + 2, :], pb[:, :].rearrange("p (h t) -> p h t", h=2),
                            cs_t[:].unsqueeze(1).to_broadcast([128, 2, TT]), ALU.mult, [bpb, b_cs], [b_mq])
                self.dma("pool", d["MQ"][:, :, t0:t0 + TT].rearrange("h p t -> p h t"), mq[:], [b_mq], [bd["MQ"]], "so9")
                knope_and_mv(ckn, b_ckn, kt0)

            load(0)
            load(1)
            part_A(0)
            for i in range(NTILE):
                part_B(i)
                part_C(i)
                part_D(i)
                if i + 1 < NTILE:
                    part_A(i + 1)
                part_E(i)
                if i + 2 < NTILE:
                    load(i + 2)
            P.emit()

    def attn_finish(self, po, bpo, N, rden, b_rden, ost, b_ost, dst, dst_buf, grp):
        self.P.add("dve", lambda e: e.reciprocal(out=rden[0:64, 0:N], in_=po[64:128, 0:N]), reads=[bpo], writes=[b_rden])
        self.tt("dve", ost[0:64, 0:N], po[0:64, 0:N], rden[0:64, 0:N], ALU.mult, [bpo, b_rden], [b_ost])
        self.dma("pool", dst, ost[0:64, 0:N], [b_ost], [dst_buf], grp)

    def phase_mla_sample(self, l, MW):
        P, d, bd = self.P, self.d, self.bd
        NKC = (NS + 256) // 128
        NKEY = NS + 256
        with ExitStack() as st:
            kh = [self.sb(st, "kh%d" % i, [128, NKEY], BF16) for i in range(2)]
            vh = [self.sb(st, "vh%d" % i, [128, NKC, 128], BF16) for i in range(2)]
            qh = [self.sb(st, "qh%d" % i, [128, 512], BF16) for i in range(3)]
            pt = [self.sb(st, "pt%d" % i, [128, 2, 512], BF16) for i in range(3)]
            rden = [self.sb(st, "rden%d" % i, [64, 512], F32) for i in range(2)]
            ost = [self.sb(st, "ost%d" % i, [64, 512], BF16) for i in range(2)]
            b_kh = [P.buf("kh0"), P.buf("kh1")]
            b_vh = [P.buf("vh0"), P.buf("vh1")]
            b_qh = [P.buf("qh%d" % i) for i in range(3)]
            b_pt = [P.buf("pt%d" % i) for i in range(3)]
            b_rden = [P.buf("rd0"), P.buf("rd1")]
            b_ost = [P.buf("os0"), P.buf("os1")]
            if MW is not None:
                self.load_merge_weights(l, MW)

            def load_head(h):
                s = h % 2
                self.dma("sp", kh[s][0:64, :], d["KN"][h * 64:(h + 1) * 64, 0:NKEY], [bd["KN"]], [b_kh[s]], "kl%d" % s)
                self.dma("sp", kh[s][64:128, :], d["KR"][:, 0:NKEY], [bd["KR"]], [b_kh[s]], "kl%d" % s)
                mvv = d["MV"][0:NKEY, h * 128:(h + 1) * 128].rearrange("(c p) f -> p c f", p=128)
                for q in range(2):
                    self.dma("sp", vh[s][:, q * 17:(q + 1) * 17, :], mvv[:, q * 17:(q + 1) * 17, :], [bd["MV"]], [b_vh[s]], "vl%d" % s)

            NCP = NKC // 2
            LOOK = 2
            steps = [(h * 8 + t, h, t, cp_) for h in range(8) for t in range(8) for cp_ in range(NCP)]

            def load_q(it):
                h, t = divmod(it, 8)
                qs = it % 3
                self.dma("sp", qh[qs][:], d["MQ"][h][:, t * 512:(t + 1) * 512], [bd["MQ"]], [b_qh[qs]], "ql%d" % qs)

            def emit_qk(k):
                it, h, t, cp_ = steps[k]
                s, qs, w = h % 2, it % 3, k % 3
                psw = self.psw[w]
                for hh in range(2):
                    c = 2 * cp_ + hh
                    self.mm(psw[:, hh, :], kh[s][:, c * 128:(c + 1) * 128], qh[qs][:], True, True,
                            [b_kh[s], b_qh[qs]], [self.bps[2 * w + hh]])

            def emit_rest(k):
                it, h, t, cp_ = steps[k]
                s, w, pi = h % 2, k % 3, k % 3
                psw = self.psw[w]
                po, bpo = self.ps(6 + it % 2)
                self.act(pt[pi][:], psw[:], AF.Exp, [self.bps[2 * w], self.bps[2 * w + 1]], [b_pt[pi]], scale=MLA_SCALE)
                for hh in range(2):
                    c = 2 * cp_ + hh
                    self.mm(po[:, :], vh[s][:, c, :], pt[pi][:, hh, :], c == 0, c == NKC - 1, [b_vh[s], b_pt[pi]], [bpo])
                if cp_ == NCP - 1:
                    self.attn_finish(po, bpo, 512, rden[it % 2], b_rden[it % 2], ost[it % 2], b_ost[it % 2],
                                     d["MO"][h * 64:(h + 1) * 64, t * 512:(t + 1) * 512], bd["MO"], "mo%d" % (it % 2))

            load_head(0)
            load_q(0)
            for k in range(min(LOOK, len(steps))):
                emit_qk(k)
            for k in range(len(steps)):
                it, h, t, cp_ = steps[k]
                if cp_ == 0:
                    if t == 0 and h + 1 < 8:
                        load_head(h + 1)
                    if it + 1 < 64:
                        load_q(it + 1)
                if k + LOOK < len(steps):
                    emit_qk(k + LOOK)
                emit_rest(k)
            P.emit()

    def phase_na_sample(self, l, E):
        P, d, bd = self.P, self.d, self.bd
        plan = _na_plan()
        with ExitStack() as st:
            kw = [self.sb(st, "kw%d" % i, [128, 4, 1024], BF16) for i in range(2)]
            vw = [self.sb(st, "vw%d" % i, [128, 8, 1024], BF16) for i in range(2)]
            kc_ = self.sb(st, "kc", [128, 4, 256], BF16)
            vc = self.sb(st, "vc", [128, 2, 1024], BF16)
            qt = [self.sb(st, "qt%d" % i, [128, 4, 512], BF16) for i in range(2)]
            ptc = [self.sb(st, "ptc%d" % i, [128, 2, 512], BF16) for i in range(3)]
            et = [self.sb(st, "et%d" % i, [128, 512], BF16) for i in range(6)]
            pt = [self.sb(st, "pt%d" % i, [128, 512], BF16) for i in range(6)]
            rden = [self.sb(st, "rden%d" % i, [64, 512], F32) for i in range(2)]
            ost = [self.sb(st, "ost%d" % i, [64, 512], BF16) for i in range(2)]
            b_E = P.buf("E")
            ident = self.sb(st, "ident", [128, 128], BF16)
            b_id = P.buf("ident")
            self.dma("sp", ident[:], d["ident"], [], [b_id], "c0")
            b_kw = [P.buf("kw0"), P.buf("kw1")]
            b_vw = [P.buf("vw0"), P.buf("vw1")]
            b_kc, b_vc = P.buf("kc"), P.buf("vc")
            b_qt = [P.buf("qt0"), P.buf("qt1")]
            b_ptc = [P.buf("ptc%d" % i) for i in range(3)]
            b_et = [P.buf("et%d" % i) for i in range(6)]
            b_pt = [P.buf("pt%d" % i) for i in range(6)]
            b_rden = [P.buf("rd0"), P.buf("rd1")]
            b_ost = [P.buf("os0"), P.buf("os1")]
            self.dma("sp", kc_[:], d["NAK"][:, KCTX:KCTX + 256].rearrange("(j p) t -> p j t", p=128), [bd["NAK"]], [b_kc], "c1")
            self.dma("sp", vc[:], d["NAV"][KCTX:KCTX + 256, :].rearrange("(c p) f -> p c f", p=128), [bd["NAV"]], [b_vc], "c2")

            def load(t):
                s = t % 2
                ks = plan[t][0]
                self.dma("sp", kw[s][:], d["NAK"][:, ks * 64:ks * 64 + 1024].rearrange("(j p) t -> p j t", p=128),
                         [bd["NAK"]], [b_kw[s]], "kl%d" % s)
                self.dma("sp", vw[s][:], d["NAV"][ks * 64:ks * 64 + 1024, :].rearrange("(c p) f -> p c f", p=128),
                         [bd["NAV"]], [b_vw[s]], "vl%d" % s)
                self.dma("sp", qt[s][:], d["NAQ"][:, t * 512:(t + 1) * 512].rearrange("(j p) t -> p j t", p=128),
                         [bd["NAQ"]], [b_qt[s]], "ql%d" % s)

            LOOK = 2
            NSLOT = 3
            steps = []
            for t in range(8):
                for h in range(8):
                    live = [c for c in range(8) if plan[t][1][c] is not None]
                    steps.append((t * 8 + h, t, h, -1, False))
                    steps.append((t * 8 + h, t, h, -2, False))
                    for c in live:
                        steps.append((t * 8 + h, t, h, c, c == live[-1]))

            def emit_qk(k):
                it, t, h, c, last = steps[k]
                s, w = t % 2, k % NSLOT
                j, a = divmod(h, 2)
                rows = slice(a * 64, (a + 1) * 64)
                pb, bpb = self.ps(w)
                if c < 0:
                    cc = -c - 1
                    self.mm(pb[:, :], kc_[rows, j, cc * 128:(cc + 1) * 128], qt[s][rows, j, :], True, True,
                            [b_kc, b_qt[s]], [bpb])
                else:
                    b0, b1, segs = plan[t][1][c]
                    c0, c1 = b0 * 64, b1 * 64
                    self.mm(pb[:, c0:c1], kw[s][rows, j, c * 128:(c + 1) * 128], qt[s][rows, j, c0:c1], True, False,
                            [b_kw[s], b_qt[s]], [bpb])
                    for si, (sb0, sb1, kind, jj0) in enumerate(segs):
                        nb = sb1 - sb0
                        ev = E[:, h * 2 + kind, jj0 * 64:(jj0 + nb) * 64]
                        self.mm(pb[:, sb0 * 64:sb1 * 64], ident[:], ev, False, si == len(segs) - 1, [b_E, b_id], [bpb])

            def emit_rest(k):
                it, t, h, c, last = steps[k]
                s, w, e_i = t % 2, k % NSLOT, k % NSLOT
                pb, bpb = self.ps(w)
                po, bpo = self.ps(3 + it % 2)
                if c < 0:
                    cc = -c - 1
                    self.act(pt[e_i][:], pb[:, :], AF.Exp, [bpb], [b_pt[e_i]])
                    self.mm(po[:, :], vc[:, cc, h * 128:(h + 1) * 128], pt[e_i][:], cc == 0, False, [b_vc, b_pt[e_i]], [bpo])
                else:
                    b0, b1, segs = plan[t][1][c]
                    c0, c1 = b0 * 64, b1 * 64
                    self.act(pt[e_i][:, c0:c1], pb[:, c0:c1], AF.Exp, [bpb], [b_pt[e_i]])
                    self.mm(po[:, c0:c1], vw[s][:, c, h * 128:(h + 1) * 128], pt[e_i][:, c0:c1], False, last,
                            [b_vw[s], b_pt[e_i]], [bpo])
                if last:
                    self.attn_finish(po, bpo, 512, rden[it % 2], b_rden[it % 2], ost[it % 2], b_ost[it % 2],
                                     d["NAO"][h * 64:(h + 1) * 64, t * 512:(t + 1) * 512], bd["NAO"], "no%d" % (it % 2))

            p_init, p_step, p_n = self.phase_prompt_attn(l, st)
            load(0)
            p_init()
            for k in range(min(LOOK, len(steps))):
                emit_qk(k)
            pk = 0
            for k in range(len(steps)):
                it, t, h, c, last = steps[k]
                if c == -1 and h == 0 and t + 1 < 8:
                    load(t + 1)
                if k + LOOK < len(steps):
                    emit_qk(k + LOOK)
                emit_rest(k)
                if last and pk < p_n:
                    p_step(pk)
                    pk += 1
            while pk < p_n:
                p_step(pk)
                pk += 1
            P.emit()

    def phase_prompt_attn(self, l, st):
        P, d, bd = self.P, self.d, self.bd
        if True:
            kp = [self.sb(st, "kp%d" % i, [128, 4, 256], BF16) for i in range(2)]
            vp = [self.sb(st, "vp%d" % i, [128, 2, 1024], BF16) for i in range(2)]
            qp = [self.sb(st, "qp%d" % i, [128, 4, 256], BF16) for i in range(2)]
            km = [self.sb(st, "km%d" % i, [128, 8, 256], BF16) for i in range(2)]
            vm = [self.sb(st, "vm%d" % i, [128, 2, 1024], BF16) for i in range(2)]
            qm = [self.sb(st, "qm%d" % i, [128, 8, 256], BF16) for i in range(2)]
            pt = [self.sb(st, "pt%d" % i, [128, 2, 256], BF16) for i in range(4)]
            rden = [self.sb(st, "rden%d" % i, [64, 256], F32) for i in range(2)]
            ost = [self.sb(st, "ost%d" % i, [64, 256], BF16) for i in range(2)]
            names = ("kp", "vp", "qp", "km", "vm", "qm")
            bb = {n: [P.buf(n + "0"), P.buf(n + "1")] for n in names}
            b_pt = [P.buf("pt%d" % i) for i in range(4)]
            b_rden = [P.buf("rd0"), P.buf("rd1")]
            b_ost = [P.buf("os0"), P.buf("os1")]

            def load(sq_):
                s = sq_ % 2
                kt0 = KPR + sq_ * 256
                t0 = NS + sq_ * 256
                self.dma("sp", kp[s][:], d["NAK"][:, kt0:kt0 + 256].rearrange("(j p) t -> p j t", p=128), [bd["NAK"]], [bb["kp"][s]], "a%d" % s)
                self.dma("sp", vp[s][:], d["NAV"][kt0:kt0 + 256, :].rearrange("(c p) f -> p c f", p=128), [bd["NAV"]], [bb["vp"][s]], "b%d" % s)
                self.dma("sp", qp[s][:], d["NAQ"][:, t0:t0 + 256].rearrange("(j p) t -> p j t", p=128), [bd["NAQ"]], [bb["qp"][s]], "c%d" % s)
                self.dma("sp", km[s][0:64, :, :], d["KN"][:, kt0:kt0 + 256].rearrange("(h p) t -> p h t", p=64), [bd["KN"]], [bb["km"][s]], "d%d" % s)
                for h in range(8):
                    self.dma("sp", km[s][64:128, h, :], d["KR"][:, kt0:kt0 + 256], [bd["KR"]], [bb["km"][s]], "d%d" % s)
                self.dma("sp", vm[s][:], d["MV"][kt0:kt0 + 256, :].rearrange("(c p) f -> p c f", p=128), [bd["MV"]], [bb["vm"][s]], "e%d" % s)
                self.dma("sp", qm[s][:], d["MQ"][:, :, t0:t0 + 256].rearrange("h p t -> p h t"), [bd["MQ"]], [bb["qm"][s]], "f%d" % s)

            LOOK = 1
            steps = [(sq_ * 16 + kind * 8 + h, sq_, kind, h) for sq_ in range(4) for kind in range(2) for h in range(8)]

            def emit_qk(k):
                it, sq_, kind, h = steps[k]
                s = sq_ % 2
                pb, bpb = self.ps(5 + k % 2)
                for c in range(2):
                    if kind == 0:
                        j, a = divmod(h, 2)
                        rows = slice(a * 64, (a + 1) * 64)
                        self.mm(pb[:, c * 256:(c + 1) * 256], kp[s][rows, j, c * 128:(c + 1) * 128], qp[s][rows, j, :], True, True,
                                [bb["kp"][s], bb["qp"][s]], [bpb])
                    else:
                        self.mm(pb[:, c * 256:(c + 1) * 256], km[s][:, h, c * 128:(c + 1) * 128], qm[s][:, h, :], True, True,
                                [bb["km"][s], bb["qm"][s]], [bpb])

            def emit_rest(k):
                it, sq_, kind, h = steps[k]
                s = sq_ % 2
                t0 = NS + sq_ * 256
                pb, bpb = self.ps(5 + k % 2)
                po7, bpo = self.ps(7)
                po = po7[:, (it % 2) * 256:(it % 2 + 1) * 256]
                pi = k % 4
                self.act(pt[pi][:].rearrange("p c t -> p (c t)"), pb[:, :], AF.Exp, [bpb], [b_pt[pi]],
                         scale=(1.0 if kind == 0 else MLA_SCALE))
                vsrc, bv = (vp[s], bb["vp"][s]) if kind == 0 else (vm[s], bb["vm"][s])
                for c in range(2):
                    self.mm(po, vsrc[:, c, h * 128:(h + 1) * 128], pt[pi][:, c, :], c == 0, c == 1, [bv, b_pt[pi]], [bpo])
                dst = d["NAO"] if kind == 0 else d["MO"]
                self.attn_finish(po, bpo, 256, rden[it % 2], b_rden[it % 2], ost[it % 2], b_ost[it % 2],
                                 dst[h * 64:(h + 1) * 64, t0:t0 + 256], bd["NAO" if kind == 0 else "MO"], "po%d" % (it % 2))

            def init():
                load(0)
                for k in range(LOOK):
                    emit_qk(k)

            def step(k):
                it, sq_, kind, h = steps[k]
                if kind == 0 and h == 0 and sq_ + 1 < 4:
                    load(sq_ + 1)
                if k + LOOK < len(steps):
                    emit_qk(k + LOOK)
                emit_rest(k)

            return init, step, len(steps)

    def load_merge_weights(self, l, MW):
        P, d = self.P, self.d
        b = {"wg": [P.buf("wg%d" % k) for k in range(8)], "wbr": [P.buf("wbr%d" % k) for k in range(3)],
             "wo": [P.buf("wo0"), P.buf("wo1")], "pw": P.buf("pw"), "bands": P.buf("bands")}
        wv = d["w_in"][l].rearrange("(k p) n -> p k n", p=128)
        self.dma("sp", MW["bands"][:].rearrange("p g k t -> p (g k t)"), d["bands"], [], [b["bands"]], "c0")
        self.dma("pool", MW["pw"][:], d["pool_w"][l].rearrange("g c n -> c g n"), [], [b["pw"]], "w")
        for k in range(3):
            self.dma("pool", MW["wbr"][:, 4 * k:4 * k + 4, :], d["w_branch"][l, k].rearrange("(c p) n -> p c n", p=128), [], [b["wbr"][k]], "w")
        for kc in range(8):
            self.dma("pool", MW["wg"][:, kc, :], wv[:, kc, 2592:5664], [], [b["wg"][kc]], "w")
        for k2 in range(2):
            self.dma("pool", MW["wo"][:, 4 * k2:4 * k2 + 4, :],
                     d["w_out"][l].rearrange("(c p) n -> p c n", p=128)[:, 4 * k2:4 * k2 + 4, :], [], [b["wo"][k2]], "w")
        return b

    def phase_merge(self, l, MW, load_w=False):
        P, d, bd = self.P, self.d, self.bd
        bc = self.b_const
        with ExitStack() as st:
            wg, wbr, wo, pw, bands = MW["wg"], MW["wbr"], MW["wo"], MW["pw"], MW["bands"]
            xt = [self.sb(st, "xt%d" % i, [128, 8, TT], F32) for i in range(2)]
            ut = [self.sb(st, "ut%d" % i, [128, 8, TT], BF16) for i in range(2)]
            br = [self.sb(st, "br%d" % i, [128, 3, 4, TT], BF16) for i in range(2)]
            pinh = [self.sb(st, "pinh%d" % i, [128, 4, 512], BF16) for i in range(2)]
            yb = self.sb(st, "yb", [128, 4, 128], BF16)
            sgm = [self.sb(st, "sgm%d" % i, [128, TT], F32) for i in range(2)]
            pr = [self.sb(st, "pr%d" % i, [128, TT], F32) for i in range(3)]
            acc = self.sb(st, "acc", [128, TT], F32)
            m = self.sb(st, "m", [128, 8, TT], BF16)
            ys = self.sb(st, "ys", [128, 8, TT], F32)
            ysq = self.sb(st, "ysq", [128, 8, TT], BF16)
            rs2 = self.sb(st, "rs2", [128, TT], F32)
            if load_w:
                bw = self.load_merge_weights(l, MW)
            else:
                bw = {"wg": [P.buf("wg%d" % k) for k in range(8)], "wbr": [P.buf("wbr%d" % k) for k in range(3)],
                      "wo": [P.buf("wo0"), P.buf("wo1")], "pw": P.buf("pw"), "bands": P.buf("bands")}
            b_wg, b_wbr3, b_wo2, b_pw, b_bands = bw["wg"], bw["wbr"], bw["wo"], bw["pw"], bw["bands"]
            b_xt = [P.buf("xt0"), P.buf("xt1")]
            b_ut = [P.buf("ut0"), P.buf("ut1")]
            b_br = [[P.buf("br%d_%d" % (i, k)) for k in range(3)] for i in range(2)]
            b_pinh = [P.buf("pinh0"), P.buf("pinh1")]
            b_yb = P.buf("yb")
            b_sgm = [P.buf("sgm0"), P.buf("sgm1")]
            b_pr = [P.buf("pr%d" % i) for i in range(3)]
            b_acc, b_m, b_ys, b_ysq, b_rs2 = [P.buf(n) for n in ("acc", "m", "ys", "ysq", "rs2")]
            pscale = lambda g: self.vecs[:, 256 + l * 4 + g:256 + l * 4 + g + 1]

            def blk_info(i, blk):
                if i < 16:
                    ob = 2 * i + blk
                    return ob == 0, ob == 31
                return blk == 0, blk == 1

            def load(i):
                s = i % 2
                t0 = i * TT
                self.dma("sp", xt[s][:], d["yT"][:, t0:t0 + TT].rearrange("(k p) t -> p k t", p=128), [bd["X"]], [b_xt[s]], "xl%d" % s)
                self.dma("sp", ut[s][:], d["U"][:, t0:t0 + TT].rearrange("(k p) t -> p k t", p=128), [bd["U"]], [b_ut[s]], "ul%d" % s)
                self.dma("sp", br[s][:, 0, :, :], d["NAO"][:, t0:t0 + TT].rearrange("(c p) t -> p c t", p=128), [bd["NAO"]], [b_br[s][0]], "nl%d" % s)
                self.dma("sp", br[s][:, 2, :, :], d["MO"][:, t0:t0 + TT].rearrange("(c p) t -> p c t", p=128), [bd["MO"]], [b_br[s][2]], "ml%d" % s)
                f0, _ = blk_info(i, 0)
                _, l1 = blk_info(i, 1)
                lo = 1 if f0 else 0
                hi = 3 if l1 else 4
                r0 = t0 - 128 + lo * 128
                self.dma("sp", pinh[s][:, lo:hi, :], d["PIN"][r0:r0 + (hi - lo) * 128, :].rearrange("(b p) f -> p b f", p=128),
                         [bd["PIN"]], [b_pinh[s]], "pl%d" % s)

            load(0)
            for i in range(NTILE):
                if i + 1 < NTILE:
                    load(i + 1)
                s = i % 2
                v = 0 if i < 16 else 1
                t0 = i * TT
                for blk in range(2):
                    first, last = blk_info(i, blk)
                    pb, bpb = self.ps(0)
                    for g in range(4):
                        rels = [r for r in (-1, 0, 1) if not ((r == -1 and first) or (r == 1 and last))]
                        for ri, r in enumerate(rels):
                            kind = 0 if r == -1 else (1 if r == 1 else (3 if first else (4 if last else 2)))
                            self.mm(pb[:, g * 128:(g + 1) * 128], pinh[s][:, blk + 1 + r, g * 128:(g + 1) * 128], bands[:, g, kind, :],
                                    ri == 0, ri == len(rels) - 1, [b_pinh[s], b_bands], [bpb])
                    self.cp("act", yb[:].rearrange("p g t -> p (g t)"), pb[:, :], [bpb], [b_yb])
                    pb2, bpb2 = self.ps(1)
                    for g in range(4):
                        self.mm(pb2[:, g * 128:(g + 1) * 128], pw[:, g, :], yb[:, g, :], True, True, [b_pw, b_yb], [bpb2])
                    for g in range(4):
                        self.act(br[s][:, 1, g, blk * 128:(blk + 1) * 128], pb2[:, g * 128:(g + 1) * 128], AF.Identity, [bpb2, bc],
                                 [b_br[s][1]], scale=pscale(g))
                it = 0
                for dc in range(8):
                    for k in range(3):
                        pb, bpb = self.ps(2 + it % 2)
                        for kc in range(4):
                            self.mm(pb[:, 0:TT], wbr[:, 4 * k + kc, dc * 128:(dc + 1) * 128], br[s][:, k, kc, :], kc == 0, kc == 3,
                                    [b_wbr3[k], b_br[s][k]], [bpb])
                        for kc in range(8):
                            self.mm(pb[:, TT:2 * TT], wg[:, kc, k * 1024 + dc * 128:k * 1024 + (dc + 1) * 128], ut[s][:, kc, :], kc == 0, kc == 7,
                                    [b_wg[kc], b_ut[s]], [bpb])
                        self.act(sgm[it % 2][:], pb[:, TT:2 * TT], AF.Sigmoid, [bpb], [b_sgm[it % 2]])
                        self.tt("dve", pr[k][:], sgm[it % 2][:], pb[:, 0:TT], ALU.mult, [b_sgm[it % 2], bpb], [b_pr[k]])
                        it += 1
                    self.tt("pool", acc[:], pr[0][:], pr[1][:], ALU.add, [b_pr[0], b_pr[1]], [b_acc])
                    self.tt("pool", m[:, dc, :], acc[:], pr[2][:], ALU.add, [b_acc, b_pr[2]], [b_m])
                for oc in range(8):
                    pb, bpb = self.ps(4 + oc % 2)
                    for kc in range(8):
                        self.mm(pb[:, 0:TT], wo[:, kc, oc * 128:(oc + 1) * 128], m[:, kc, :], kc == 0, kc == 7, [b_wo2[kc // 4], b_m], [bpb])
                    self.act(ys[:, oc, :], pb[:, 0:TT], AF.Identity, [bpb, bc], [b_ys], scale=self.mod_G(l, 1, v, oc))
                    self.act(ysq[:, oc, :], pb[:, 0:TT], AF.Square, [bpb], [b_ysq])
                self.postnorm_residual(l, 1, v, ys, b_ys, ysq, b_ysq, rs2, b_rs2, xt[s], b_xt[s], 7)
                self.dma("pool", d["yT"][:, t0:t0 + TT].rearrange("(k p) t -> p k t", p=128), xt[s][:], [b_xt[s]], [bd["X"]], "xs%d" % s)
            P.emit()


_NC_CACHE = {}


def _get_nc():
    if "nc" not in _NC_CACHE:
        _NC_CACHE["nc"] = Builder().build()
    return _NC_CACHE["nc"]


def _chunk_vec(v):
    return np.ascontiguousarray(np.asarray(v, np.float32).reshape(-1, 128).T)


def kernel(x_prompt, x_sample, cache_na_k, cache_na_v, cache_mla_ckv, cache_mla_krope, c, c_ctx,
           w_ada, b_ada, norm_pre, norm_post, ffn_w_in, ffn_w_out, w_in, na_rpb, pool_w, pool_scale,
           mla_q_norm, mla_kv_norm, mla_w_uq, mla_w_ukv, w_branch, w_out):
    f32 = lambda a: np.ascontiguousarray(np.asarray(a, dtype=np.float32))
    x_prompt, x_sample = f32(x_prompt), f32(x_sample)
    w_ada, ffn_w_in, ffn_w_out, w_in = f32(w_ada), f32(ffn_w_in), f32(ffn_w_out), f32(w_in)
    w_branch, w_out, pool_w = f32(w_branch), f32(w_out), f32(pool_w)
    mla_w_uq, mla_w_ukv, na_rpb = f32(mla_w_uq), f32(mla_w_ukv), f32(na_rpb)
    w_kr = w_in[:, :, 2560:2592]
    w_kr2 = np.ascontiguousarray(np.concatenate([np.zeros((2, D, 64), np.float32), w_kr, w_kr[:, :, ROPE_PERM]], axis=2))
    uq = mla_w_uq.reshape(2, 256, 8, 96)
    w_uq2 = np.ascontiguousarray(np.concatenate([uq[..., 0:64], uq[..., 64:96], uq[..., 64:96][..., ROPE_PERM]], axis=-1).reshape(2, 256, 1024))
    ukv = mla_w_ukv.reshape(2, 256, 8, 128)
    w_ukv_k = np.ascontiguousarray(ukv[..., 0:64].reshape(2, 256, 512))
    w_ukv_v = np.ascontiguousarray(ukv[..., 64:128].reshape(2, 256, 512))
    badd = np.ascontiguousarray(_rpb_tables(na_rpb).reshape(2, 128, 16, 1536))
    bands = np.ascontiguousarray(_bands().reshape(128, 20 * 128))
    fold = _fold()
    cs = _cs_table()
    shared = {"cs": cs, "w_ada": w_ada, "ffn_w_in": ffn_w_in, "ffn_w_out": ffn_w_out, "w_in": w_in, "w_kr2": w_kr2,
              "w_uq2": w_uq2, "w_ukv_k": w_ukv_k, "w_ukv_v": w_ukv_v, "pool_w": pool_w, "w_branch": w_branch,
              "w_out": w_out, "badd": badd, "bands": bands, "fold": fold,
              "ident": np.eye(128, dtype=np.float32).astype(ml_dtypes.bfloat16)}
    vec_common = [None]
    for l in range(2):
        vec_common.append(_chunk_vec(b_ada[l]))
    vec_common.append(np.concatenate([_chunk_vec(np.asarray(norm_pre)[l, j]) for l in range(2) for j in range(3)], axis=1))
    vec_common.append(np.concatenate([_chunk_vec(np.asarray(norm_post)[l, j]) for l in range(2) for j in range(3)], axis=1))
    vec_common.append(np.concatenate([_chunk_vec(np.asarray(pool_scale)[l]) for l in range(2)], axis=1))
    vec_common.append(np.concatenate([_chunk_vec(np.asarray(mla_q_norm)[l]) for l in range(2)], axis=1))
    vec_common.append(np.concatenate([_chunk_vec(np.asarray(mla_kv_norm)[l]) for l in range(2)], axis=1))
    c = f32(c)
    c_ctx = f32(c_ctx)
    in_maps = []
    for i in range(8):
        xs = x_sample[i].T
        xp = x_prompt[4 * i:4 * i + 4].reshape(NPR, D).T
        xT = np.ascontiguousarray(np.concatenate([xs, xp], axis=1))
        c2 = np.stack([_chunk_vec(c[i]), _chunk_vec(c_ctx)], axis=2).reshape(128, 16)
        vecs = np.ascontiguousarray(np.concatenate([c2] + vec_common[1:], axis=1).astype(np.float32))
        assert vecs.shape == (128, NVEC), vecs.shape
        m = dict(shared)
        m["xT"] = xT
        m["vecs"] = vecs
        m["c_kT"] = np.ascontiguousarray(f32(cache_na_k[i]).reshape(2, 256, 512).transpose(0, 2, 1))
        m["c_v"] = np.ascontiguousarray(f32(cache_na_v[i]).reshape(2, 256, 512))
        m["c_ckvT"] = np.ascontiguousarray(f32(cache_mla_ckv[i]).transpose(0, 2, 1))
        m["c_krT"] = np.ascontiguousarray(f32(cache_mla_krope[i]).transpose(0, 2, 1))
        in_maps.append(m)
    nc = _get_nc()
    res = run_bass_kernel_spmd(nc, in_maps, core_ids=list(range(8)))
    rs = res.results
    y_prompt = np.empty((32, 256, D), np.float32)
    y_sample = np.empty((8, NS, D), np.float32)
    st_k = np.empty((32, 2, 256, 8, 64), np.float32)
    st_v = np.empty((32, 2, 256, 8, 64), np.float32)
    st_ckv = np.empty((32, 2, 256, 256), np.float32)
    st_kr = np.empty((32, 2, 256, 32), np.float32)
    for i in range(8):
        r = rs[i]
        yT = np.asarray(r["yT"])
        y_sample[i] = yT[:, :NS].T
        y_prompt[4 * i:4 * i + 4] = yT[:, NS:].T.reshape(4, 256, D)
        k = np.asarray(r["st_k"])
        st_k[4 * i:4 * i + 4] = k.transpose(2, 0, 1).reshape(4, 256, 2, 8, 64).transpose(0, 2, 1, 3, 4)
        vv = np.asarray(r["st_v"])
        st_v[4 * i:4 * i + 4] = vv.reshape(2, 4, 256, 8, 64).transpose(1, 0, 2, 3, 4)
        ck = np.asarray(r["st_ckv"])
        st_ckv[4 * i:4 * i + 4] = ck.transpose(2, 0, 1).reshape(4, 256, 2, 256).transpose(0, 2, 1, 3)
        kr = np.asarray(r["st_kr"])
        st_kr[4 * i:4 * i + 4] = kr.transpose(2, 0, 1).reshape(4, 256, 2, 32).transpose(0, 2, 1, 3)
    return (y_prompt, y_sample, st_k, st_v, st_ckv, st_kr)
```

```python
import math
from contextlib import ExitStack

import ml_dtypes
import numpy as np

import concourse.bass as bass
import concourse.mybir as mybir
from concourse.bass_utils import run_bass_kernel_spmd

F32 = mybir.dt.float32
BF16 = mybir.dt.bfloat16
AF = mybir.ActivationFunctionType
ALU = mybir.AluOpType

D = 1024
NS = 4096
NPR = 1024
NT = NS + NPR
KT = NS + 256 + NPR
KCTX = NS
KPR = NS + 256
TT = 256
NTILE = NT // TT
FF = 2816
NF = FF // 128
EPS = 1e-6
POOL_W = (2, 4, 8, 16)
NVEC = 272
NA_SCALE = 64 ** -0.5
MLA_SCALE = 96 ** -0.5


class Buf:
    __slots__ = ("name", "lw", "rd", "psum")

    def __init__(self, name, psum=False):
        self.name = name
        self.lw = None
        self.rd = []
        self.psum = psum


class Op:
    __slots__ = ("eng", "fn", "deps", "dma", "needed", "sem", "val")

    def __init__(self, eng, fn, dma):
        self.eng = eng
        self.fn = fn
        self.dma = dma
        self.deps = []
        self.needed = dma is not None
        self.sem = None
        self.val = 0


class Prog:
    ENGS = ("pe", "act", "dve", "pool", "sp")

    def __init__(self, nc, gstack):
        self.nc = nc
        self.gstack = gstack
        self.sems = {}
        self.counts = {}
        self.bufs = []
        self.nbar = 0
        self.nops = 0
        self._reset()

    def _reset(self):
        self.ops = {e: [] for e in self.ENGS}
        self.grp_last = {}
        for b in self.bufs:
            b.lw = None
            b.rd = []

    def buf(self, name, psum=False):
        b = Buf(name, psum)
        self.bufs.append(b)
        return b

    def get_sem(self, key):
        if key not in self.sems:
            self.sems[key] = self.gstack.enter_context(self.nc.semaphore("s%d" % len(self.sems)))
        return self.sems[key]

    def add(self, eng, fn, reads=(), writes=(), dma=None):
        op = Op(eng, fn, dma)
        self.nops += 1
        deps = {}

        def dep(d, kind):
            if d is None or d is op:
                return
            if d.dma is None and dma is None and d.eng == eng and eng != "pool":
                if kind != "raw" or eng == "pe":
                    return
            deps[id(d)] = d

        for b in reads:
            dep(b.lw, "raw")
            if b.psum:
                for r in b.rd:
                    if r.eng != eng or r.dma is not None:
                        dep(r, "rar")
        for b in writes:
            dep(b.lw, "waw")
            for r in b.rd:
                dep(r, "war")
        if dma is not None:
            prev = self.grp_last.get(dma)
            if prev is not None:
                deps[id(prev)] = prev
            self.grp_last[dma] = op
        op.deps = list(deps.values())
        for d in op.deps:
            d.needed = True
        wset = set(id(b) for b in writes)
        for b in writes:
            b.lw = op
            b.rd = []
        for b in reads:
            if id(b) not in wset:
                b.rd.append(op)
        self.ops[eng].append(op)
        return op

    def emit(self):
        nc = self.nc
        import os
        if os.environ.get("KDEBUG"):
            print("phase", self.nbar, "sbuf remaining", nc.sbuf_bytes_remaining, "ops", {e: len(v) for e, v in self.ops.items()})
        final_deps = []
        for e in self.ENGS:
            last = None
            for o in self.ops[e]:
                if o.dma is None:
                    last = o
            if last is not None:
                last.needed = True
                final_deps.append(last)
        for g, o in self.grp_last.items():
            final_deps.append(o)
        for e in self.ENGS:
            for o in self.ops[e]:
                if o.dma is not None:
                    key = ("d", o.dma)
                    self.counts[key] = self.counts.get(key, 0) + 16
                    o.sem, o.val = key, self.counts[key]
                    self.get_sem(key)
                elif o.needed:
                    key = ("e", e)
                    self.counts[key] = self.counts.get(key, 0) + 1
                    o.sem, o.val = key, self.counts[key]
                    self.get_sem(key)
        bar = self.get_sem(("bar",))
        self.nbar += 1
        nbar = self.nbar
        sems = self.sems
        ops = self.ops

        def run(e, engobj):
            known = {}
            for o in ops[e]:
                need = {}
                for d in o.deps:
                    if d.val > need.get(d.sem, 0):
                        need[d.sem] = d.val
                for s, v in need.items():
                    if known.get(s, 0) >= v:
                        continue
                    engobj.wait_ge(sems[s], v)
                    known[s] = v
                ins = o.fn(engobj)
                if o.sem is not None:
                    ins.then_inc(sems[o.sem], 16 if o.dma is not None else 1)
            if e == "sp":
                need = {}
                for d in final_deps:
                    if d.val > need.get(d.sem, 0):
                        need[d.sem] = d.val
                for s, v in need.items():
                    if known.get(s, 0) >= v:
                        continue
                    engobj.wait_ge(sems[s], v)
                engobj.sem_inc(bar, 1)
            else:
                engobj.wait_ge(bar, nbar)

        with nc.Block() as block:
            @block.tensor
            def _(pe):
                run("pe", pe)

            @block.scalar
            def _(act):
                run("act", act)

            @block.vector
            def _(dve):
                run("dve", dve)

            @block.gpsimd
            def _(pool):
                run("pool", pool)

            @block.sync
            def _(sp):
                run("sp", sp)
        self._reset()


ROPE_PERM = np.array(list(range(8, 16)) + list(range(0, 8)) + list(range(24, 32)) + list(range(16, 24)))


def _cs_table():
    cs = np.ones((128, NT), np.float64)
    t = np.arange(NS)
    row = (t // 64).astype(np.float64)
    col = (t % 64).astype(np.float64)
    inv = 10000.0 ** (-np.arange(8, dtype=np.float64) / 8)
    for d in range(32):
        pos = row if d < 16 else col
        ang = (pos.astype(np.float32) * inv[d % 8].astype(np.float32)).astype(np.float32)
        cs[64 + d, :NS] = np.cos(ang)
        sgn = -1.0 if (d % 16) < 8 else 1.0
        cs[96 + d, :NS] = sgn * np.sin(ang)
    cs[64:96, NS:] = 1.0
    cs[96:128, NS:] = 0.0
    return cs.astype(np.float32)


def _fold():
    f = np.zeros((128, 128), np.float32)
    for k in range(64, 128):
        for m in range(64, 128):
            if (k - 64) % 32 == (m - 64) % 32:
                f[k, m] = 1.0
    return f.astype(ml_dtypes.bfloat16)


def _bands():
    b = np.zeros((128, 4, 5, 128), np.float64)
    tp = np.arange(128)[:, None]
    t = np.arange(128)[None, :]
    for g, w in enumerate(POOL_W):
        hw = w // 2
        b[:, g, 0, :] = np.where(tp >= 128 + t - hw, 1.0 / w, 0.0)
        b[:, g, 1, :] = np.where(tp + 128 < t + hw, 1.0 / w, 0.0)
        inwin = (tp >= t - hw) & (tp < t + hw)
        eye = (tp == t).astype(np.float64)
        b[:, g, 2, :] = np.where(inwin, 1.0 / w, 0.0) - eye
        lo = np.maximum(t - hw, 0)
        cnt = (t + hw) - lo
        b[:, g, 3, :] = np.where(inwin, 1.0 / cnt, 0.0) - eye
        hi = np.minimum(t + hw, 128)
        cnt = hi - (t - hw)
        b[:, g, 4, :] = np.where(inwin, 1.0 / cnt, 0.0) - eye
    return b.astype(np.float32).astype(ml_dtypes.bfloat16)


def _rpb_tables(na_rpb):
    L = na_rpb.shape[0]
    p = np.arange(128)
    a = (p // 64)[:, None, None]
    kc = (p % 64)[:, None, None]
    jj = np.arange(24)[None, :, None]
    qc = np.arange(64)[None, None, :]
    relr = a + 18 - jj + 0 * qc + 0 * kc
    ws = np.clip(qc - 8, 0, 48)
    inwin = (kc >= ws) & (kc < ws + 16)
    cidx = np.clip(kc - qc + 15, 0, 30) + 0 * jj
    out = np.full((L, 128, 8, 2, 24, 64), -1e30, np.float32)
    for kind, (lo, hi) in enumerate(((0, 14), (3, 10))):
        ok = (relr >= lo) & (relr <= hi) & inwin
        rr = np.clip(relr, 0, 14)
        for l in range(L):
            for h in range(8):
                vals = na_rpb[l, h][rr, cidx]
                out[l, :, h, kind] = np.where(ok, vals, np.float32(-1e30))
    return out


def _na_plan():
    plan = []
    for t in range(8):
        ks = min(max(8 * t - 4, 0), 48)
        chunks = []
        for c in range(8):
            kr0 = ks + 2 * c
            Dd = kr0 - 8 * t
            rows = []
            for b in range(8):
                qr = 8 * t + b
                if qr < 4:
                    ok, kind = (kr0 <= 6), 0
                elif qr > 60:
                    ok, kind = (kr0 >= 56), 0
                else:
                    rs = qr - 4
                    ok = any(rs <= kr < rs + 8 for kr in (kr0, kr0 + 1))
                    kind = 1
                rows.append((ok, kind))
            valid = [b for b in range(8) if rows[b][0]]
            if not valid:
                chunks.append(None)
                continue
            b0, b1 = valid[0], valid[-1] + 1
            assert all(rows[b][0] for b in range(b0, b1))
            segs = []
            b = b0
            while b < b1:
                e = b
                while e < b1 and rows[e][1] == rows[b][1]:
                    e += 1
                jj0 = b - Dd + 11
                assert 0 <= jj0 and jj0 + (e - b) <= 24
                segs.append((b, e, rows[b][1], jj0))
                b = e
            chunks.append((b0, b1, segs))
        plan.append((ks, chunks))
    return plan


class Builder:
    def __init__(self):
        self.nc = bass.Bass("TRN2", target_bir_lowering=False)
        self.gst = ExitStack()

    def din(self, name, shape, dt=F32):
        return self.nc.dram_tensor(name, list(shape), dt, kind="ExternalInput").ap()

    def dout(self, name, shape, dt=F32):
        return self.nc.dram_tensor(name, list(shape), dt, kind="ExternalOutput").ap()

    def dscr(self, name, shape, dt=BF16):
        return self.nc.dram_tensor(name, list(shape), dt, kind="Internal").ap()

    def build(self):
        nc = self.nc
        with self.gst:
            self._build()
        return nc

    def sb(self, st, name, shape, dt):
        self.uid = getattr(self, "uid", 0) + 1
        return st.enter_context(self.nc.sbuf_tensor("sb%d_%s" % (self.uid, name), list(shape), dt))

    def dma(self, eng, out, in_, reads, writes, grp):
        if grp in ("w", "wa"):
            self.wrot = getattr(self, "wrot", 0) + 1
            grp = "w%d" % (self.wrot % 4)
        self.P.add(eng, lambda e: e.dma_start(out=out, in_=in_), reads=reads, writes=writes, dma=grp)

    def mm(self, out, lhsT, rhs, start, stop, reads, writes):
        self.P.add("pe", lambda e: e.matmul(out, lhsT=lhsT, rhs=rhs, start=start, stop=stop),
                   reads=reads, writes=writes)

    def act(self, out, in_, func, reads, writes, scale=None, bias=None):
        kw = {}
        if scale is not None:
            kw["scale"] = scale
        if bias is not None:
            kw["bias"] = bias
        self.P.add("act", lambda e: e.activation(out=out, in_=in_, func=func, **kw), reads=reads, writes=writes)

    def tt(self, eng, out, in0, in1, op, reads, writes):
        self.P.add(eng, lambda e: e.tensor_tensor(out=out, in0=in0, in1=in1, op=op), reads=reads, writes=writes)

    def ts(self, eng, out, in0, s1, s2, op0, op1, reads, writes):
        self.P.add(eng, lambda e: e.tensor_scalar(out=out, in0=in0, scalar1=s1, scalar2=s2, op0=op0, op1=op1),
                   reads=reads, writes=writes)

    def stt(self, eng, out, in0, scalar, in1, op0, op1, reads, writes):
        self.P.add(eng, lambda e: e.scalar_tensor_tensor(out=out, in0=in0, scalar=scalar, in1=in1, op0=op0, op1=op1),
                   reads=reads, writes=writes)

    def cp(self, eng, out, in_, reads, writes):
        if eng == "act":
            self.P.add("act", lambda e: e.copy(out=out, in_=in_), reads=reads, writes=writes)
        else:
            self.P.add(eng, lambda e: e.tensor_copy(out=out, in_=in_), reads=reads, writes=writes)

    def rstd_from(self, out, ps_ap, reads, writes):
        self.act(out, ps_ap, AF.Sqrt, reads, writes, scale=1.0, bias=EPS)
        self.P.add("dve", lambda e: e.reciprocal(out=out, in_=out), reads=writes, writes=writes)

    def _build(self):
        nc = self.nc
        gst = self.gst
        P = self.P = Prog(nc, gst)
        d = self.d = {}
        d["xT"] = self.din("xT", [D, NT])
        d["vecs"] = self.din("vecs", [128, NVEC])
        d["cs"] = self.din("cs", [128, NT])
        d["w_ada"] = self.din("w_ada", [2, D, 9 * D])
        d["ffn_w_in"] = self.din("ffn_w_in", [2, 2, D, 2 * FF])
        d["ffn_w_out"] = self.din("ffn_w_out", [2, 2, FF, D])
        d["w_in"] = self.din("w_in", [2, D, 5664])
        d["w_kr2"] = self.din("w_kr2", [2, D, 128])
        d["w_uq2"] = self.din("w_uq2", [2, 256, 1024])
        d["w_ukv_k"] = self.din("w_ukv_k", [2, 256, 512])
        d["w_ukv_v"] = self.din("w_ukv_v", [2, 256, 512])
        d["pool_w"] = self.din("pool_w", [2, 4, 128, 128])
        d["w_branch"] = self.din("w_branch", [2, 3, 512, D])
        d["w_out"] = self.din("w_out", [2, D, D])
        d["badd"] = self.din("badd", [2, 128, 16, 1536])
        d["bands"] = self.din("bands", [128, 20 * 128], BF16)
        d["fold"] = self.din("fold", [128, 128], BF16)
        d["ident"] = self.din("ident", [128, 128], BF16)
        d["c_kT"] = self.din("c_kT", [2, 512, 256])
        d["c_v"] = self.din("c_v", [2, 256, 512])
        d["c_ckvT"] = self.din("c_ckvT", [2, 256, 256])
        d["c_krT"] = self.din("c_krT", [2, 32, 256])
        d["yT"] = self.dout("yT", [D, NT])
        d["st_k"] = self.dout("st_k", [2, 512, NPR])
        d["st_v"] = self.dout("st_v", [2, NPR, 512])
        d["st_ckv"] = self.dout("st_ckv", [2, 256, NPR])
        d["st_kr"] = self.dout("st_kr", [2, 32, NPR])
        d["U"] = self.dscr("U", [D, NT])
        d["NAQ"] = self.dscr("NAQ", [512, NT])
        d["NAK"] = self.dscr("NAK", [512, KT])
        d["NAV"] = self.dscr("NAV", [KT, 1024])
        d["KN"] = self.dscr("KN", [512, KT])
        d["KR"] = self.dscr("KR", [64, KT])
        d["MV"] = self.dscr("MV", [KT, 1024])
        d["MQ"] = self.dscr("MQ", [8, 128, NT])
        d["PIN"] = self.dscr("PIN", [NT, 512])
        d["NAO"] = self.dscr("NAO", [512, NT])
        d["MO"] = self.dscr("MO", [512, NT])
        self.bd = {k: P.buf("d_" + k) for k in ("X", "U", "NAQ", "NAK", "NAV", "KN", "KR", "MV", "MQ", "PIN", "NAO", "MO", "OUT")}

        self.vecs = self.sb(gst, "vecs", [128, NVEC], F32)
        self.mods = self.sb(gst, "mods", [128, 2, 72, 2], F32)
        self.AG = self.sb(gst, "AG", [128, 2, 3, 2, 2, 8], F32)
        self.onesD = self.sb(gst, "onesD", [128, 128], BF16)
        self.ones256 = self.sb(gst, "ones256", [128, 128], BF16)
        self.b_const = P.buf("const")
        self.psw = [gst.enter_context(nc.psum_tensor("psw%d" % i, [128, 2, 512], F32)) for i in range(4)]
        self.bps = [P.buf("ps%d" % i, psum=True) for i in range(8)]

        self.phase_setup()
        for l in range(2):
            self.phase_ffn(l, 0)
            with ExitStack() as sE:
                E = self.sb(sE, "E", [128, 16, 1536], BF16)
                self.phase_inproj(l, E)
                self.phase_na_sample(l, E)
            self.phase_mla_sample(l, None)
            with ExitStack() as sW:
                MW = {"wg": self.sb(sW, "wg", [128, 8, 3072], BF16), "wbr": self.sb(sW, "wbr", [128, 12, D], BF16),
                      "wo": self.sb(sW, "wo", [128, 8, D], BF16), "pw": self.sb(sW, "pw", [128, 4, 128], BF16),
                      "bands": self.sb(sW, "bands", [128, 4, 5, 128], BF16)}
                self.phase_merge(l, MW, load_w=True)
            self.phase_ffn(l, 1)

    def ps(self, i):
        return self.psw[i // 2][:, i % 2, :], self.bps[i]

    def phase_setup(self):
        P, d = self.P, self.d
        with ExitStack() as st:
            sc = self.sb(st, "sc", [128, 8, 2], BF16)
            wa = [self.sb(st, "wa%d" % i, [128, 8, 2304], BF16) for i in range(2)]
            b_sc = P.buf("sc")
            b_wa = [[P.buf("wa%d_%d" % (i, k)) for k in range(8)] for i in range(2)]
            b_mods = P.buf("mods")
            bc = self.b_const
            self.dma("sp", self.vecs[:], d["vecs"], [], [bc], "c0")
            P.add("dve", lambda e: e.memset(self.onesD[:], 1.0 / 1024), writes=[bc])
            P.add("dve", lambda e: e.memset(self.ones256[:], 1.0 / 256), writes=[bc])
            self.act(sc[:].rearrange("p k v -> p (k v)"), self.vecs[:, 0:16], AF.Silu, [bc], [b_sc])
            psm, bpsm = self.ps(0)
            for l in range(2):
                for pc in range(4):
                    wi = (l * 4 + pc) % 2
                    for kc in range(8):
                        self.dma("pool", wa[wi][:, kc, :], d["w_ada"][l, kc * 128:(kc + 1) * 128, pc * 2304:(pc + 1) * 2304],
                                 [], [b_wa[wi][kc]], "wa")
                    for mm_ in range(18):
                        m = pc * 18 + mm_
                        for kc in range(8):
                            self.mm(psm[:, 2 * m:2 * m + 2], wa[wi][:, kc, mm_ * 128:(mm_ + 1) * 128], sc[:, kc, :],
                                    kc == 0, kc == 7, [b_wa[wi][kc], b_sc], [bpsm])
                psv = psm[:, 0:144].rearrange("p (m v) -> p m v", v=2)
                for v in range(2):
                    self.tt("dve", self.mods[:, l, :, v], psv[:, :, v], self.vecs[:, 16 + l * 72:16 + (l + 1) * 72], ALU.add,
                            [bpsm, bc], [b_mods])
                for j in range(3):
                    cj = 1.0 if j == 1 else 0.5
                    for v in range(2):
                        npre = self.vecs[:, 160 + l * 24 + j * 8:160 + l * 24 + j * 8 + 8]
                        npost = self.vecs[:, 208 + l * 24 + j * 8:208 + l * 24 + j * 8 + 8]
                        self.stt("dve", self.AG[:, l, j, v, 0, :], self.mods[:, l, (3 * j + 1) * 8:(3 * j + 2) * 8, v], 1.0, npre,
                                 ALU.add, ALU.mult, [b_mods, bc], [bc])
                        self.stt("dve", self.AG[:, l, j, v, 1, :], self.mods[:, l, (3 * j + 2) * 8:(3 * j + 3) * 8, v], cj, npost,
                                 ALU.mult, ALU.mult, [b_mods, bc], [bc])
            P.emit()

    def mod_A(self, l, j, v, kc):
        return self.AG[:, l, j, v, 0, kc:kc + 1]

    def mod_G(self, l, j, v, kc):
        return self.AG[:, l, j, v, 1, kc:kc + 1]

    def mod_B(self, l, j, v, kc):
        return self.mods[:, l, 3 * j * 8 + kc, v:v + 1]

    def xsrc(self, l, which):
        return self.d["xT"] if (l == 0 and which == 0) else self.d["yT"]

    def prenorm(self, l, j, v, xb, b_xb, sq, b_sq, rstd, b_rstd, t2, b_t2, u, b_u, stat_bank, T=TT, square=True, rest=True):
        bc = self.b_const
        if square:
            self.act(sq[:], xb[:], AF.Square, [b_xb], [b_sq])
        if not rest:
            return
        pst, bpst = self.ps(stat_bank)
        for kc in range(8):
            self.mm(pst[:, 0:T], self.onesD[:], sq[:, kc, :], kc == 0, kc == 7, [bc, b_sq], [bpst])
        self.rstd_from(rstd[:], pst[:, 0:T], [bpst], [b_rstd])
        self.tt("dve", t2[:], xb[:], rstd[:].unsqueeze(1).to_broadcast([128, 8, T]), ALU.mult, [b_xb, b_rstd], [b_t2])
        for kc in range(8):
            self.act(u[:, kc, :], t2[:, kc, :], AF.Identity, [b_t2, bc], [b_u],
                     scale=self.mod_A(l, j, v, kc), bias=self.mod_B(l, j, v, kc))

    def postnorm_residual(self, l, j, v, ys, b_ys, ysq, b_ysq, rstd, b_rstd, xb, b_xb, stat_bank, T=TT):
        bc = self.b_const
        pst, bpst = self.ps(stat_bank)
        for kc in range(8):
            self.mm(pst[:, 0:T], self.onesD[:], ysq[:, kc, :], kc == 0, kc == 7, [bc, b_ysq], [bpst])
        self.rstd_from(rstd[:], pst[:, 0:T], [bpst], [b_rstd])
        self.tt("dve", ys[:], ys[:], rstd[:].unsqueeze(1).to_broadcast([128, 8, T]), ALU.mult, [b_ys, b_rstd], [b_ys])
        self.tt("dve", xb[:], xb[:], ys[:], ALU.add, [b_xb, b_ys], [b_xb])

    def phase_ffn(self, l, which):
        P, d = self.P, self.d
        j = 0 if which == 0 else 2
        xsrc = self.xsrc(l, which)
        xdst = d["yT"]
        bX = self.bd["X"]
        with ExitStack() as st:
            win = self.sb(st, "win", [128, 8, 2 * FF], BF16)
            wout = self.sb(st, "wout", [128, NF, D], BF16)
            xt = [self.sb(st, "xt%d" % i, [128, 8, TT], F32) for i in range(3)]
            u = self.sb(st, "u", [128, 8, TT], BF16)
            sq = self.sb(st, "sq", [128, 8, TT], BF16)
            ysq = self.sb(st, "ysq", [128, 8, TT], BF16)
            t2 = self.sb(st, "t2", [128, 8, TT], F32)
            ys = self.sb(st, "ys", [128, 8, TT], F32)
            g = self.sb(st, "g", [128, NF, TT], BF16)
            sg = [self.sb(st, "sg%d" % i, [128, TT], F32) for i in range(2)]
            rs1 = self.sb(st, "rs1", [128, TT], F32)
            rs2 = self.sb(st, "rs2", [128, TT], F32)
            FG = [0, 6, 12, 17, 22]
            b_win = [[[P.buf("win%d_%d_%d" % (gi, k, hf)) for hf in range(2)] for k in range(8)] for gi in range(4)]

            def fgrp(f):
                for gi in range(4):
                    if FG[gi] <= f < FG[gi + 1]:
                        return gi
            b_wout = [P.buf("wout%d" % k) for k in range(4)]
            b_xt = [P.buf("xt0"), P.buf("xt1"), P.buf("xt2")]
            b_u, b_sq, b_t2, b_ys = [P.buf(n) for n in ("u", "sq", "t2", "ys")]
            b_ysq = [P.buf("ysq%d" % k) for k in range(8)]
            b_g = [P.buf("g%d" % f) for f in range(NF)]
            b_sg = [P.buf("sg0"), P.buf("sg1")]
            b_rs1, b_rs2 = P.buf("rs1"), P.buf("rs2")
            bc = self.b_const
            wi_v = d["ffn_w_in"][l, which].rearrange("(k p) (h c) -> p k h c", p=128, h=2)
            win_v = win[:].rearrange("p k (h c) -> p k h c", h=2)
            for gi in range(4):
                for kc in range(8):
                    c0, c1 = FG[gi] * 128, FG[gi + 1] * 128
                    self.dma("pool", win_v[:, kc, :, c0:c1], wi_v[:, kc, :, c0:c1],
                             [], [b_win[gi][kc][0], b_win[gi][kc][1]], "w")
            wo_v = d["ffn_w_out"][l, which].rearrange("(f p) n -> p f n", p=128)
            fsp = [0, 6, 12, 17, 22]
            for q in range(4):
                self.dma("pool", wout[:, fsp[q]:fsp[q + 1], :], wo_v[:, fsp[q]:fsp[q + 1], :], [], [b_wout[q]], "w")

            def wout_buf(f):
                for q in range(4):
                    if fsp[q] <= f < fsp[q + 1]:
                        return b_wout[q]

            def load(i):
                self.dma("sp", xt[i % 3][:], xsrc[:, i * TT:(i + 1) * TT].rearrange("(k p) t -> p k t", p=128),
                         [bX], [b_xt[i % 3]], "xl%d" % (i % 3))

            def sqpart(i):
                v = 0 if i < 16 else 1
                self.prenorm(l, j, v, xt[i % 3], b_xt[i % 3], sq, b_sq, rs1, b_rs1, t2, b_t2, u, b_u, 6, rest=False)

            def pro_b(i):
                v = 0 if i < 16 else 1
                self.prenorm(l, j, v, xt[i % 3], b_xt[i % 3], sq, b_sq, rs1, b_rs1, t2, b_t2, u, b_u, 6, square=False)

            def hpart(i, hook):
                for f in range(NF):
                    bank = f % 4
                    pb, bpb = self.ps(bank)
                    for half in range(2):
                        for kc in range(8):
                            self.mm(pb[:, half * TT:(half + 1) * TT], win[:, kc, half * FF + f * 128:half * FF + (f + 1) * 128],
                                    u[:, kc, :], kc == 0, kc == 7, [b_win[fgrp(f)][kc][half], b_u], [bpb])
                    self.act(sg[f % 2][:], pb[:, 0:TT], AF.Silu, [bpb], [b_sg[f % 2]])
                    self.tt("dve", g[:, f, :], sg[f % 2][:], pb[:, TT:2 * TT], ALU.mult, [b_sg[f % 2], bpb], [b_g[f]])
                    if f == 1 and hook is not None:
                        hook()

            pst2, bpst2 = self.ps(7)

            def ypart(i):
                v = 0 if i < 16 else 1
                for dc in range(8):
                    pb, bpb = self.ps(4 + dc % 2)
                    for f in range(NF):
                        self.mm(pb[:, 0:TT], wout[:, f, dc * 128:(dc + 1) * 128], g[:, f, :], f == 0, f == NF - 1,
                                [wout_buf(f), b_g[f]], [bpb])
                    if dc >= 1:
                        self.mm(pst2[:, 0:TT], self.onesD[:], ysq[:, dc - 1, :], dc == 1, False, [bc, b_ysq[dc - 1]], [bpst2])
                    self.act(ysq[:, dc, :], pb[:, 0:TT], AF.Square, [bpb], [b_ysq[dc]])
                    self.act(ys[:, dc, :], pb[:, 0:TT], AF.Identity, [bpb, bc], [b_ys], scale=self.mod_G(l, j, v, dc))

            def epilogue(i):
                xb, b_xb = xt[i % 3], b_xt[i % 3]
                self.mm(pst2[:, 0:TT], self.onesD[:], ysq[:, 7, :], False, True, [bc, b_ysq[7]], [bpst2])
                self.rstd_from(rs2[:], pst2[:, 0:TT], [bpst2], [b_rs2])
                self.tt("dve", ys[:], ys[:], rs2[:].unsqueeze(1).to_broadcast([128, 8, TT]), ALU.mult, [b_ys, b_rs2], [b_ys])
                self.tt("dve", xb[:], xb[:], ys[:], ALU.add, [b_xb, b_ys], [b_xb])
                self.dma("pool", xdst[:, i * TT:(i + 1) * TT].rearrange("(k p) t -> p k t", p=128), xb[:],
                         [b_xb], [bX], "xs%d" % (i % 3))

            load(0)
            load(1)
            sqpart(0)
            pro_b(0)
            for i in range(NTILE):
                if i + 1 < NTILE:
                    sqpart(i + 1)

                def hook(i=i):
                    if i >= 1:
                        epilogue(i - 1)
                    if i + 2 < NTILE:
                        load(i + 2)

                hpart(i, hook)
                if i + 1 < NTILE:
                    pro_b(i + 1)
                ypart(i)
            epilogue(NTILE - 1)
            P.emit()

    def phase_inproj(self, l, E):
        P, d = self.P, self.d
        bd = self.bd
        bc = self.b_const
        with ExitStack() as st:
            wqk = self.sb(st, "wqk", [128, 8, 1024], BF16)
            wvp = self.sb(st, "wvp", [128, 8, 1024], BF16)
            wc = self.sb(st, "wc", [128, 8, 512], BF16)
            wkr = self.sb(st, "wkr", [128, 8, 128], BF16)
            wuq = self.sb(st, "wuq", [128, 2, 1024], BF16)
            wuk = self.sb(st, "wuk", [128, 2, 512], BF16)
            wuv = self.sb(st, "wuv", [128, 2, 512], BF16)
            fold = self.sb(st, "fold", [128, 128], BF16)
            xt = [self.sb(st, "xt%d" % i, [128, 8, TT], F32) for i in range(2)]
            cst = [self.sb(st, "cst%d" % i, [128, TT], F32) for i in range(2)]
            u = self.sb(st, "u", [128, 8, TT], BF16)
            sq = self.sb(st, "sq", [128, 8, TT], BF16)
            t2 = self.sb(st, "t2", [128, 8, TT], F32)
            rs1 = self.sb(st, "rs1", [128, TT], F32)
            qko = self.sb(st, "qko", [128, 8, TT], BF16)
            kst = self.sb(st, "kst", [128, 4, TT], F32)
            csb = self.sb(st, "csb", [128, 2, 2, TT], F32)
            csq = self.sb(st, "csq", [128, 2, 2, TT], BF16)
            rsc = self.sb(st, "rsc", [128, 2, TT], F32)
            cqn = self.sb(st, "cqn", [128, 2, TT], BF16)
            ckf = self.sb(st, "ckf", [128, 2, TT], F32)
            ckn = self.sb(st, "ckn", [128, 2, TT], BF16)
            mq = self.sb(st, "mq", [128, 8, TT], BF16)
            kn = self.sb(st, "kn", [64, 8, TT], BF16)
            krst = self.sb(st, "krst", [128, TT], F32)
            krt = self.sb(st, "krt", [128, TT], BF16)
            krd = self.sb(st, "krd", [128, TT], BF16)
            vaug = self.sb(st, "vaug", [128, 2, 8, 128], BF16)
            vst = self.sb(st, "vst", [128, 2, 512], F32)
            pin = self.sb(st, "pin", [128, 2, 512], BF16)
            mvaug = self.sb(st, "mvaug", [128, 2, 8, 128], BF16)
            stg = self.sb(st, "stg", [128, 4, 256], F32)
            b_w = {k: P.buf("w_" + k) for k in ("qk", "vp", "c", "kr", "uq", "uk", "uv", "fold")}
            b_xt = [P.buf("xt0"), P.buf("xt1")]
            b_cst = [P.buf("cst0"), P.buf("cst1")]
            (b_u, b_sq, b_t2, b_rs1, b_qko, b_kst, b_csb_, b_csq_, b_rsc, b_cqn, b_ckf, b_ckn, b_mq, b_kn, b_krst, b_krt,
             b_krd, b_vaug, b_vst, b_pin, b_mvaug, b_stg) = [P.buf("b%d" % i) for i in range(22)]
            b_csb = [P.buf("csb0"), P.buf("csb1")]
            b_csq = [P.buf("csq0"), P.buf("csq1")]
            wv = d["w_in"][l].rearrange("(k p) n -> p k n", p=128)
            for k2 in range(2):
                ks = slice(4 * k2, 4 * k2 + 4)
                self.dma("pool", wqk[:, ks, :], wv[:, ks, 0:1024], [], [b_w["qk"]], "w")
                self.dma("pool", wvp[:, ks, :], wv[:, ks, 1024:2048], [], [b_w["vp"]], "w")
            self.dma("pool", wc[:], wv[:, :, 2048:2560], [], [b_w["c"]], "w")
            self.dma("pool", wkr[:], d["w_kr2"][l].rearrange("(k p) n -> p k n", p=128), [], [b_w["kr"]], "w")
            self.dma("pool", wuq[:], d["w_uq2"][l].rearrange("(k p) n -> p k n", p=128), [], [b_w["uq"]], "w")
            self.dma("pool", wuk[:], d["w_ukv_k"][l].rearrange("(k p) n -> p k n", p=128), [], [b_w["uk"]], "w")
            self.dma("pool", wuv[:], d["w_ukv_v"][l].rearrange("(k p) n -> p k n", p=128), [], [b_w["uv"]], "w")
            self.dma("sp", fold[:], d["fold"], [], [b_w["fold"]], "c0")
            P.add("dve", lambda e: e.memset(vaug[:], 1.0), writes=[b_vaug])
            P.add("dve", lambda e: e.memset(mvaug[:], 1.0), writes=[b_mvaug])
            for i4 in range(4):
                self.dma("pool", E[:, 4 * i4:4 * i4 + 4, :], d["badd"][l, :, 4 * i4:4 * i4 + 4, :], [], [P.buf("E%d" % i4)], "w")
            qn = lambda kc: self.vecs[:, 264 + l * 2 + kc:264 + l * 2 + kc + 1]
            kvn = lambda kc: self.vecs[:, 268 + l * 2 + kc:268 + l * 2 + kc + 1]
            rot = [0]

            def bank():
                rot[0] = (rot[0] + 1) % 6
                return self.ps(rot[0])

            def evac_eng():
                return "act" if rot[0] % 2 == 0 else "dve"

            def knope_and_mv(ckn_t, b_ckn_t, kt0):
                for hp in range(4):
                    pb, bpb = bank()
                    for hh in range(2):
                        h = 2 * hp + hh
                        for kc in range(2):
                            self.mm(pb[0:64, hh * TT:(hh + 1) * TT], wuk[:, kc, h * 64:(h + 1) * 64], ckn_t[:, kc, :],
                                    kc == 0, kc == 1, [b_w["uk"], b_ckn_t], [bpb])
                    self.cp(evac_eng(), kn[:, 2 * hp:2 * hp + 2, :], pb[0:64, :].rearrange("p (h t) -> p h t", h=2),
                            [bpb], [b_kn])
                self.dma("pool", d["KN"][:, kt0:kt0 + TT].rearrange("(h p) t -> p h t", p=64), kn[:], [b_kn], [bd["KN"]], "so1")
                for blk in range(2):
                    pb, bpb = bank()
                    for kc in range(2):
                        self.mm(pb[:, :], ckn_t[:, kc, blk * 128:(blk + 1) * 128], wuv[:, kc, :], kc == 0, kc == 1,
                                [b_w["uv"], b_ckn_t], [bpb])
                    self.cp(evac_eng(), mvaug[:, blk, :, 0:64], pb[:, :].rearrange("p (h e) -> p h e", h=8), [bpb], [b_mvaug])
                self.dma("pool", d["MV"][kt0:kt0 + TT, :].rearrange("(b p) f -> p b f", p=128),
                         mvaug[:].rearrange("p b h e -> p b (h e)"), [b_mvaug], [bd["MV"]], "so2")

            kt0 = KCTX
            self.dma("sp", stg[:], d["c_kT"][l].rearrange("(c p) t -> p c t", p=128), [], [b_stg], "cx")
            self.cp("dve", qko[:, 4:8, :], stg[:], [b_stg], [b_qko])
            self.dma("pool", d["NAK"][:, kt0:kt0 + TT].rearrange("(c p) t -> p c t", p=128), qko[:, 4:8, :], [b_qko], [bd["NAK"]], "so3")
            self.dma("sp", stg[:].rearrange("p (b x) t -> p b (x t)", b=2), d["c_v"][l].rearrange("(b p) f -> p b f", p=128),
                     [b_qko], [b_stg], "cx")
            self.cp("dve", vaug[:, :, :, 0:64], stg[:].rearrange("p (b x) t -> p b (x t)", b=2).rearrange("p b (h e) -> p b h e", h=8),
                    [b_stg], [b_vaug])
            self.dma("pool", d["NAV"][kt0:kt0 + TT, :].rearrange("(b p) f -> p b f", p=128),
                     vaug[:].rearrange("p b h e -> p b (h e)"), [b_vaug], [bd["NAV"]], "so4")
            self.dma("sp", stg[:, 0:2, :], d["c_ckvT"][l].rearrange("(c p) t -> p c t", p=128), [b_vaug], [b_stg], "cx")
            self.cp("dve", ckn[:], stg[:, 0:2, :], [b_stg], [b_ckn])
            knope_and_mv(ckn, b_ckn, kt0)
            self.dma("sp", stg[64:96, 2, :], d["c_krT"][l], [], [b_stg], "cx")
            self.dma("sp", stg[96:128, 2, :], d["c_krT"][l], [], [b_stg], "cx")
            self.cp("dve", krd[64:128, :], stg[64:128, 2, :], [b_stg], [b_krd])
            self.dma("pool", d["KR"][:, kt0:kt0 + TT], krd[64:128, :], [b_krd], [bd["KR"]], "so5")

            xsrc = d["yT"]

            def load(i):
                self.dma("sp", xt[i % 2][:], xsrc[:, i * TT:(i + 1) * TT].rearrange("(k p) t -> p k t", p=128),
                         [bd["X"]], [b_xt[i % 2]], "xl%d" % (i % 2))
                self.dma("sp", cst[i % 2][:], d["cs"][:, i * TT:(i + 1) * TT], [], [b_cst[i % 2]], "cl%d" % (i % 2))

            def tinfo(i):
                v = 0 if i < 16 else 1
                t0 = i * TT
                kt0 = t0 if i < 16 else KPR + (i - 16) * TT
                pt0 = (i - 16) * TT
                return v, t0, kt0, pt0

            def part_A(i):
                v, t0, kt0, pt0 = tinfo(i)
                self.prenorm(l, 1, v, xt[i % 2], b_xt[i % 2], sq, b_sq, rs1, b_rs1, t2, b_t2, u, b_u, 6)
                self.dma("pool", d["U"][:, t0:t0 + TT].rearrange("(k p) t -> p k t", p=128), u[:], [b_u], [bd["U"]], "so0")

            def part_B(i):
                v, t0, kt0, pt0 = tinfo(i)
                for cpair in range(4):
                    pb, bpb = bank()
                    for hh in range(2):
                        c = 2 * cpair + hh
                        for kc in range(8):
                            self.mm(pb[:, hh * TT:(hh + 1) * TT], wqk[:, kc, c * 128:(c + 1) * 128], u[:, kc, :], kc == 0, kc == 7,
                                    [b_w["qk"], b_u], [bpb])
                    pv = pb[:, :].rearrange("p (h t) -> p h t", h=2)
                    if cpair < 2:
                        self.act(qko[:, 2 * cpair:2 * cpair + 2, :], pv, AF.Copy, [bpb], [b_qko], scale=NA_SCALE)
                    else:
                        self.cp(evac_eng(), qko[:, 2 * cpair:2 * cpair + 2, :], pv, [bpb], [b_qko])
                    if v == 1 and cpair >= 2:
                        self.cp("act", kst[:, 2 * (cpair - 2):2 * (cpair - 2) + 2, :], pv, [bpb], [b_kst])
                self.dma("pool", d["NAQ"][:, t0:t0 + TT].rearrange("(c p) t -> p c t", p=128), qko[:, 0:4, :], [b_qko], [bd["NAQ"]], "so6")
                self.dma("pool", d["NAK"][:, kt0:kt0 + TT].rearrange("(c p) t -> p c t", p=128), qko[:, 4:8, :], [b_qko], [bd["NAK"]], "so3")
                if v == 1:
                    self.dma("pool", d["st_k"][l][:, pt0:pt0 + TT].rearrange("(c p) t -> p c t", p=128), kst[:], [b_kst], [bd["OUT"]], "so7")

            def part_C(i):
                v, t0, kt0, pt0 = tinfo(i)
                pbs = []
                for which in range(2):
                    pb, bpb = bank()
                    pbs.append((pb, bpb))
                    for hh in range(2):
                        c = 2 * which + hh
                        for kc in range(8):
                            self.mm(pb[:, hh * TT:(hh + 1) * TT], wc[:, kc, c * 128:(c + 1) * 128], u[:, kc, :], kc == 0, kc == 7,
                                    [b_w["c"], b_u], [bpb])
                    pv = pb[:, :].rearrange("p (h t) -> p h t", h=2)
                    self.cp("dve", csb[:, which, :, :], pv, [bpb], [b_csb[which]])
                    self.act(csq[:, which, :, :], pv, AF.Square, [bpb], [b_csq[which]])
                pst, bpst = self.ps(7)
                for which in range(2):
                    for kc in range(2):
                        self.mm(pst[:, which * TT:(which + 1) * TT], self.ones256[:], csq[:, which, kc, :], kc == 0, kc == 1,
                                [bc, b_csq[which]], [bpst])
                self.rstd_from(rsc[:].rearrange("p w t -> p (w t)"), pst[:, 0:2 * TT], [bpst], [b_rsc])
                for kc in range(2):
                    self.stt("dve", cqn[:, kc, :], csb[:, 0, kc, :], qn(kc), rsc[:, 0, :], ALU.mult, ALU.mult,
                             [b_csb[0], b_rsc, bc], [b_cqn])
                    self.stt("dve", ckf[:, kc, :], csb[:, 1, kc, :], kvn(kc), rsc[:, 1, :], ALU.mult, ALU.mult,
                             [b_csb[1], b_rsc, bc], [b_ckf])
                self.cp("act", ckn[:], ckf[:], [b_ckf], [b_ckn])
                if v == 1:
                    self.dma("pool", d["st_ckv"][l][:, pt0:pt0 + TT].rearrange("(c p) t -> p c t", p=128), ckf[:], [b_ckf], [bd["OUT"]], "so8")

            def part_D(i):
                v, t0, kt0, pt0 = tinfo(i)
                cs_t, b_cs = cst[i % 2], b_cst[i % 2]
                pb, bpb = bank()
                for kc in range(8):
                    self.mm(pb[:, 0:TT], wkr[:, kc, :], u[:, kc, :], kc == 0, kc == 7, [b_w["kr"], b_u], [bpb])
                if v == 1:
                    self.cp("act", krst[64:96, :], pb[64:96, 0:TT], [bpb], [b_krst])
                    self.dma("pool", d["st_kr"][l][:, pt0:pt0 + TT], krst[64:96, :], [b_krst], [bd["OUT"]], "so10")
                self.tt("dve", krt[:], pb[:, 0:TT], cs_t[:], ALU.mult, [bpb, b_cs], [b_krt])
                for blk in range(2):
                    pb, bpb = bank()
                    for kc in range(8):
                        self.mm(pb[:, :], u[:, kc, blk * 128:(blk + 1) * 128], wvp[:, kc, 0:512], kc == 0, kc == 7, [b_w["vp"], b_u], [bpb])
                    self.cp(evac_eng(), vaug[:, blk, :, 0:64], pb[:, :].rearrange("p (h e) -> p h e", h=8), [bpb], [b_vaug])
                    if v == 1:
                        self.cp("act", vst[:, blk, :], pb[:, :], [bpb], [b_vst])
                    pb, bpb = bank()
                    for kc in range(8):
                        self.mm(pb[:, :], u[:, kc, blk * 128:(blk + 1) * 128], wvp[:, kc, 512:1024], kc == 0, kc == 7, [b_w["vp"], b_u], [bpb])
                    self.cp(evac_eng(), pin[:, blk, :], pb[:, :], [bpb], [b_pin])
                self.dma("pool", d["NAV"][kt0:kt0 + TT, :].rearrange("(b p) f -> p b f", p=128),
                         vaug[:].rearrange("p b h e -> p b (h e)"), [b_vaug], [bd["NAV"]], "so4")
                self.dma("pool", d["PIN"][t0:t0 + TT, :].rearrange("(b p) f -> p b f", p=128), pin[:], [b_pin], [bd["PIN"]], "so11")
                if v == 1:
                    self.dma("pool", d["st_v"][l][pt0:pt0 + TT, :].rearrange("(b p) f -> p b f", p=128), vst[:], [b_vst], [bd["OUT"]], "so12")
                pb2, bpb2 = bank()
                self.mm(pb2[:, 0:TT], fold[:], krt[:], True, True, [b_w["fold"], b_krt], [bpb2])
                self.cp("act", krd[64:128, :], pb2[64:128, 0:TT], [bpb2], [b_krd])
                self.dma("pool", d["KR"][:, kt0:kt0 + TT], krd[64:128, :], [b_krd], [bd["KR"]], "so5")

            def part_E(i):
                v, t0, kt0, pt0 = tinfo(i)
                cs_t, b_cs = cst[i % 2], b_cst[i % 2]
                for hp in range(4):
                    pb, bpb = bank()
                    for hh in range(2):
                        h = 2 * hp + hh
                        for kc in range(2):
                            self.mm(pb[:, hh * TT:(hh + 1) * TT], wuq[:, kc, h * 128:(h + 1) * 128], cqn[:, kc, :], kc == 0, kc == 1,
                                    [b_w["uq"], b_cqn], [bpb])
                    self.tt("dve", mq[:, 2 * hp:2 * hp + 2, :], pb[:, :].rearrange("p (h t) -> p h t", h=2),
                            cs_t[:].unsqueeze(1).to_broadcast([128, 2, TT]), ALU.mult, [bpb, b_cs], [b_mq])
                self.dma("pool", d["MQ"][:, :, t0:t0 + TT].rearrange("h p t -> p h t"), mq[:], [b_mq], [bd["MQ"]], "so9")
                knope_and_mv(ckn, b_ckn, kt0)

            load(0)
            load(1)
            part_A(0)
            for i in range(NTILE):
                part_B(i)
                part_C(i)
                part_D(i)
                if i + 1 < NTILE:
                    part_A(i + 1)
                part_E(i)
                if i + 2 < NTILE:
                    load(i + 2)
            P.emit()

    def attn_finish(self, po, bpo, N, rden, b_rden, ost, b_ost, dst, dst_buf, grp):
        self.P.add("dve", lambda e: e.reciprocal(out=rden[0:64, 0:N], in_=po[64:128, 0:N]), reads=[bpo], writes=[b_rden])
        self.tt("dve", ost[0:64, 0:N], po[0:64, 0:N], rden[0:64, 0:N], ALU.mult, [bpo, b_rden], [b_ost])
        self.dma("pool", dst, ost[0:64, 0:N], [b_ost], [dst_buf], grp)

    def phase_mla_sample(self, l, MW):
        P, d, bd = self.P, self.d, self.bd
        NKC = (NS + 256) // 128
        NKEY = NS + 256
        with ExitStack() as st:
            kh = [self.sb(st, "kh%d" % i, [128, NKEY], BF16) for i in range(2)]
            vh = [self.sb(st, "vh%d" % i, [128, NKC, 128], BF16) for i in range(2)]
            qh = [self.sb(st, "qh%d" % i, [128, 512], BF16) for i in range(3)]
            pt = [self.sb(st, "pt%d" % i, [128, 2, 512], BF16) for i in range(3)]
            rden = [self.sb(st, "rden%d" % i, [64, 512], F32) for i in range(2)]
            ost = [self.sb(st, "ost%d" % i, [64, 512], BF16) for i in range(2)]
            b_kh = [P.buf("kh0"), P.buf("kh1")]
            b_vh = [P.buf("vh0"), P.buf("vh1")]
            b_qh = [P.buf("qh%d" % i) for i in range(3)]
            b_pt = [P.buf("pt%d" % i) for i in range(3)]
            b_rden = [P.buf("rd0"), P.buf("rd1")]
            b_ost = [P.buf("os0"), P.buf("os1")]
            if MW is not None:
                self.load_merge_weights(l, MW)

            def load_head(h):
                s = h % 2
                self.dma("sp", kh[s][0:64, :], d["KN"][h * 64:(h + 1) * 64, 0:NKEY], [bd["KN"]], [b_kh[s]], "kl%d" % s)
                self.dma("sp", kh[s][64:128, :], d["KR"][:, 0:NKEY], [bd["KR"]], [b_kh[s]], "kl%d" % s)
                mvv = d["MV"][0:NKEY, h * 128:(h + 1) * 128].rearrange("(c p) f -> p c f", p=128)
                for q in range(2):
                    self.dma("sp", vh[s][:, q * 17:(q + 1) * 17, :], mvv[:, q * 17:(q + 1) * 17, :], [bd["MV"]], [b_vh[s]], "vl%d" % s)

            NCP = NKC // 2
            LOOK = 2
            steps = [(h * 8 + t, h, t, cp_) for h in range(8) for t in range(8) for cp_ in range(NCP)]

            def load_q(it):
                h, t = divmod(it, 8)
                qs = it % 3
                self.dma("sp", qh[qs][:], d["MQ"][h][:, t * 512:(t + 1) * 512], [bd["MQ"]], [b_qh[qs]], "ql%d" % qs)

            def emit_qk(k):
                it, h, t, cp_ = steps[k]
                s, qs, w = h % 2, it % 3, k % 3
                psw = self.psw[w]
                for hh in range(2):
                    c = 2 * cp_ + hh
                    self.mm(psw[:, hh, :], kh[s][:, c * 128:(c + 1) * 128], qh[qs][:], True, True,
                            [b_kh[s], b_qh[qs]], [self.bps[2 * w + hh]])

            def emit_rest(k):
                it, h, t, cp_ = steps[k]
                s, w, pi = h % 2, k % 3, k % 3
                psw = self.psw[w]
                po, bpo = self.ps(6 + it % 2)
                self.act(pt[pi][:], psw[:], AF.Exp, [self.bps[2 * w], self.bps[2 * w + 1]], [b_pt[pi]], scale=MLA_SCALE)
                for hh in range(2):
                    c = 2 * cp_ + hh
                    self.mm(po[:, :], vh[s][:, c, :], pt[pi][:, hh, :], c == 0, c == NKC - 1, [b_vh[s], b_pt[pi]], [bpo])
                if cp_ == NCP - 1:
                    self.attn_finish(po, bpo, 512, rden[it % 2], b_rden[it % 2], ost[it % 2], b_ost[it % 2],
                                     d["MO"][h * 64:(h + 1) * 64, t * 512:(t + 1) * 512], bd["MO"], "mo%d" % (it % 2))

            load_head(0)
            load_q(0)
            for k in range(min(LOOK, len(steps))):
                emit_qk(k)
            for k in range(len(steps)):
                it, h, t, cp_ = steps[k]
                if cp_ == 0:
                    if t == 0 and h + 1 < 8:
                        load_head(h + 1)
                    if it + 1 < 64:
                        load_q(it + 1)
                if k + LOOK < len(steps):
                    emit_qk(k + LOOK)
                emit_rest(k)
            P.emit()

    def phase_na_sample(self, l, E):
        P, d, bd = self.P, self.d, self.bd
        plan = _na_plan()
        with ExitStack() as st:
            kw = [self.sb(st, "kw%d" % i, [128, 4, 1024], BF16) for i in range(2)]
            vw = [self.sb(st, "vw%d" % i, [128, 8, 1024], BF16) for i in range(2)]
            kc_ = self.sb(st, "kc", [128, 4, 256], BF16)
            vc = self.sb(st, "vc", [128, 2, 1024], BF16)
            qt = [self.sb(st, "qt%d" % i, [128, 4, 512], BF16) for i in range(2)]
            ptc = [self.sb(st, "ptc%d" % i, [128, 2, 512], BF16) for i in range(3)]
            et = [self.sb(st, "et%d" % i, [128, 512], BF16) for i in range(6)]
            pt = [self.sb(st, "pt%d" % i, [128, 512], BF16) for i in range(6)]
            rden = [self.sb(st, "rden%d" % i, [64, 512], F32) for i in range(2)]
            ost = [self.sb(st, "ost%d" % i, [64, 512], BF16) for i in range(2)]
            b_E = P.buf("E")
            ident = self.sb(st, "ident", [128, 128], BF16)
            b_id = P.buf("ident")
            self.dma("sp", ident[:], d["ident"], [], [b_id], "c0")
            b_kw = [P.buf("kw0"), P.buf("kw1")]
            b_vw = [P.buf("vw0"), P.buf("vw1")]
            b_kc, b_vc = P.buf("kc"), P.buf("vc")
            b_qt = [P.buf("qt0"), P.buf("qt1")]
            b_ptc = [P.buf("ptc%d" % i) for i in range(3)]
            b_et = [P.buf("et%d" % i) for i in range(6)]
            b_pt = [P.buf("pt%d" % i) for i in range(6)]
            b_rden = [P.buf("rd0"), P.buf("rd1")]
            b_ost = [P.buf("os0"), P.buf("os1")]
            self.dma("sp", kc_[:], d["NAK"][:, KCTX:KCTX + 256].rearrange("(j p) t -> p j t", p=128), [bd["NAK"]], [b_kc], "c1")
            self.dma("sp", vc[:], d["NAV"][KCTX:KCTX + 256, :].rearrange("(c p) f -> p c f", p=128), [bd["NAV"]], [b_vc], "c2")

            def load(t):
                s = t % 2
                ks = plan[t][0]
                self.dma("sp", kw[s][:], d["NAK"][:, ks * 64:ks * 64 + 1024].rearrange("(j p) t -> p j t", p=128),
                         [bd["NAK"]], [b_kw[s]], "kl%d" % s)
                self.dma("sp", vw[s][:], d["NAV"][ks * 64:ks * 64 + 1024, :].rearrange("(c p) f -> p c f", p=128),
                         [bd["NAV"]], [b_vw[s]], "vl%d" % s)
                self.dma("sp", qt[s][:], d["NAQ"][:, t * 512:(t + 1) * 512].rearrange("(j p) t -> p j t", p=128),
                         [bd["NAQ"]], [b_qt[s]], "ql%d" % s)

            LOOK = 2
            NSLOT = 3
            steps = []
            for t in range(8):
                for h in range(8):
                    live = [c for c in range(8) if plan[t][1][c] is not None]
                    steps.append((t * 8 + h, t, h, -1, False))
                    steps.append((t * 8 + h, t, h, -2, False))
                    for c in live:
                        steps.append((t * 8 + h, t, h, c, c == live[-1]))

            def emit_qk(k):
                it, t, h, c, last = steps[k]
                s, w = t % 2, k % NSLOT
                j, a = divmod(h, 2)
                rows = slice(a * 64, (a + 1) * 64)
                pb, bpb = self.ps(w)
                if c < 0:
                    cc = -c - 1
                    self.mm(pb[:, :], kc_[rows, j, cc * 128:(cc + 1) * 128], qt[s][rows, j, :], True, True,
                            [b_kc, b_qt[s]], [bpb])
                else:
                    b0, b1, segs = plan[t][1][c]
                    c0, c1 = b0 * 64, b1 * 64
                    self.mm(pb[:, c0:c1], kw[s][rows, j, c * 128:(c + 1) * 128], qt[s][rows, j, c0:c1], True, False,
                            [b_kw[s], b_qt[s]], [bpb])
                    for si, (sb0, sb1, kind, jj0) in enumerate(segs):
                        nb = sb1 - sb0
                        ev = E[:, h * 2 + kind, jj0 * 64:(jj0 + nb) * 64]
                        self.mm(pb[:, sb0 * 64:sb1 * 64], ident[:], ev, False, si == len(segs) - 1, [b_E, b_id], [bpb])

            def emit_rest(k):
                it, t, h, c, last = steps[k]
                s, w, e_i = t % 2, k % NSLOT, k % NSLOT
                pb, bpb = self.ps(w)
                po, bpo = self.ps(3 + it % 2)
                if c < 0:
                    cc = -c - 1
                    self.act(pt[e_i][:], pb[:, :], AF.Exp, [bpb], [b_pt[e_i]])
                    self.mm(po[:, :], vc[:, cc, h * 128:(h + 1) * 128], pt[e_i][:], cc == 0, False, [b_vc, b_pt[e_i]], [bpo])
                else:
                    b0, b1, segs = plan[t][1][c]
                    c0, c1 = b0 * 64, b1 * 64
                    self.act(pt[e_i][:, c0:c1], pb[:, c0:c1], AF.Exp, [bpb], [b_pt[e_i]])
                    self.mm(po[:, c0:c1], vw[s][:, c, h * 128:(h + 1) * 128], pt[e_i][:, c0:c1], False, last,
                            [b_vw[s], b_pt[e_i]], [bpo])
                if last:
                    self.attn_finish(po, bpo, 512, rden[it % 2], b_rden[it % 2], ost[it % 2], b_ost[it % 2],
                                     d["NAO"][h * 64:(h + 1) * 64, t * 512:(t + 1) * 512], bd["NAO"], "no%d" % (it % 2))

            p_init, p_step, p_n = self.phase_prompt_attn(l, st)
            load(0)
            p_init()
            for k in range(min(LOOK, len(steps))):
                emit_qk(k)
            pk = 0
            for k in range(len(steps)):
                it, t, h, c, last = steps[k]
                if c == -1 and h == 0 and t + 1 < 8:
                    load(t + 1)
                if k + LOOK < len(steps):
                    emit_qk(k + LOOK)
                emit_rest(k)
                if last and pk < p_n:
                    p_step(pk)
                    pk += 1
            while pk < p_n:
                p_step(pk)
                pk += 1
            P.emit()

    def phase_prompt_attn(self, l, st):
        P, d, bd = self.P, self.d, self.bd
        if True:
            kp = [self.sb(st, "kp%d" % i, [128, 4, 256], BF16) for i in range(2)]
            vp = [self.sb(st, "vp%d" % i, [128, 2, 1024], BF16) for i in range(2)]
            qp = [self.sb(st, "qp%d" % i, [128, 4, 256], BF16) for i in range(2)]
            km = [self.sb(st, "km%d" % i, [128, 8, 256], BF16) for i in range(2)]
            vm = [self.sb(st, "vm%d" % i, [128, 2, 1024], BF16) for i in range(2)]
            qm = [self.sb(st, "qm%d" % i, [128, 8, 256], BF16) for i in range(2)]
            pt = [self.sb(st, "pt%d" % i, [128, 2, 256], BF16) for i in range(4)]
            rden = [self.sb(st, "rden%d" % i, [64, 256], F32) for i in range(2)]
            ost = [self.sb(st, "ost%d" % i, [64, 256], BF16) for i in range(2)]
            names = ("kp", "vp", "qp", "km", "vm", "qm")
            bb = {n: [P.buf(n + "0"), P.buf(n + "1")] for n in names}
            b_pt = [P.buf("pt%d" % i) for i in range(4)]
            b_rden = [P.buf("rd0"), P.buf("rd1")]
            b_ost = [P.buf("os0"), P.buf("os1")]

            def load(sq_):
                s = sq_ % 2
                kt0 = KPR + sq_ * 256
                t0 = NS + sq_ * 256
                self.dma("sp", kp[s][:], d["NAK"][:, kt0:kt0 + 256].rearrange("(j p) t -> p j t", p=128), [bd["NAK"]], [bb["kp"][s]], "a%d" % s)
                self.dma("sp", vp[s][:], d["NAV"][kt0:kt0 + 256, :].rearrange("(c p) f -> p c f", p=128), [bd["NAV"]], [bb["vp"][s]], "b%d" % s)
                self.dma("sp", qp[s][:], d["NAQ"][:, t0:t0 + 256].rearrange("(j p) t -> p j t", p=128), [bd["NAQ"]], [bb["qp"][s]], "c%d" % s)
                self.dma("sp", km[s][0:64, :, :], d["KN"][:, kt0:kt0 + 256].rearrange("(h p) t -> p h t", p=64), [bd["KN"]], [bb["km"][s]], "d%d" % s)
                for h in range(8):
                    self.dma("sp", km[s][64:128, h, :], d["KR"][:, kt0:kt0 + 256], [bd["KR"]], [bb["km"][s]], "d%d" % s)
                self.dma("sp", vm[s][:], d["MV"][kt0:kt0 + 256, :].rearrange("(c p) f -> p c f", p=128), [bd["MV"]], [bb["vm"][s]], "e%d" % s)
                self.dma("sp", qm[s][:], d["MQ"][:, :, t0:t0 + 256].rearrange("h p t -> p h t"), [bd["MQ"]], [bb["qm"][s]], "f%d" % s)

            LOOK = 1
            steps = [(sq_ * 16 + kind * 8 + h, sq_, kind, h) for sq_ in range(4) for kind in range(2) for h in range(8)]

            def emit_qk(k):
                it, sq_, kind, h = steps[k]
                s = sq_ % 2
                pb, bpb = self.ps(5 + k % 2)
                for c in range(2):
                    if kind == 0:
                        j, a = divmod(h, 2)
                        rows = slice(a * 64, (a + 1) * 64)
                        self.mm(pb[:, c * 256:(c + 1) * 256], kp[s][rows, j, c * 128:(c + 1) * 128], qp[s][rows, j, :], True, True,
                                [bb["kp"][s], bb["qp"][s]], [bpb])
                    else:
                        self.mm(pb[:, c * 256:(c + 1) * 256], km[s][:, h, c * 128:(c + 1) * 128], qm[s][:, h, :], True, True,
                                [bb["km"][s], bb["qm"][s]], [bpb])

            def emit_rest(k):
                it, sq_, kind, h = steps[k]
                s = sq_ % 2
                t0 = NS + sq_ * 256
                pb, bpb = self.ps(5 + k % 2)
                po7, bpo = self.ps(7)
                po = po7[:, (it % 2) * 256:(it % 2 + 1) * 256]
                pi = k % 4
                self.act(pt[pi][:].rearrange("p c t -> p (c t)"), pb[:, :], AF.Exp, [bpb], [b_pt[pi]],
                         scale=(1.0 if kind == 0 else MLA_SCALE))
                vsrc, bv = (vp[s], bb["vp"][s]) if kind == 0 else (vm[s], bb["vm"][s])
                for c in range(2):
                    self.mm(po, vsrc[:, c, h * 128:(h + 1) * 128], pt[pi][:, c, :], c == 0, c == 1, [bv, b_pt[pi]], [bpo])
                dst = d["NAO"] if kind == 0 else d["MO"]
                self.attn_finish(po, bpo, 256, rden[it % 2], b_rden[it % 2], ost[it % 2], b_ost[it % 2],
                                 dst[h * 64:(h + 1) * 64, t0:t0 + 256], bd["NAO" if kind == 0 else "MO"], "po%d" % (it % 2))

            def init():
                load(0)
                for k in range(LOOK):
                    emit_qk(k)

            def step(k):
                it, sq_, kind, h = steps[k]
                if kind == 0 and h == 0 and sq_ + 1 < 4:
                    load(sq_ + 1)
                if k + LOOK < len(steps):
                    emit_qk(k + LOOK)
                emit_rest(k)

            return init, step, len(steps)

    def load_merge_weights(self, l, MW):
        P, d = self.P, self.d
        b = {"wg": [P.buf("wg%d" % k) for k in range(8)], "wbr": [P.buf("wbr%d" % k) for k in range(3)],
             "wo": [P.buf("wo0"), P.buf("wo1")], "pw": P.buf("pw"), "bands": P.buf("bands")}
        wv = d["w_in"][l].rearrange("(k p) n -> p k n", p=128)
        self.dma("sp", MW["bands"][:].rearrange("p g k t -> p (g k t)"), d["bands"], [], [b["bands"]], "c0")
        self.dma("pool", MW["pw"][:], d["pool_w"][l].rearrange("g c n -> c g n"), [], [b["pw"]], "w")
        for k in range(3):
            self.dma("pool", MW["wbr"][:, 4 * k:4 * k + 4, :], d["w_branch"][l, k].rearrange("(c p) n -> p c n", p=128), [], [b["wbr"][k]], "w")
        for kc in range(8):
            self.dma("pool", MW["wg"][:, kc, :], wv[:, kc, 2592:5664], [], [b["wg"][kc]], "w")
        for k2 in range(2):
            self.dma("pool", MW["wo"][:, 4 * k2:4 * k2 + 4, :],
                     d["w_out"][l].rearrange("(c p) n -> p c n", p=128)[:, 4 * k2:4 * k2 + 4, :], [], [b["wo"][k2]], "w")
        return b

    def phase_merge(self, l, MW, load_w=False):
        P, d, bd = self.P, self.d, self.bd
        bc = self.b_const
        with ExitStack() as st:
            wg, wbr, wo, pw, bands = MW["wg"], MW["wbr"], MW["wo"], MW["pw"], MW["bands"]
            xt = [self.sb(st, "xt%d" % i, [128, 8, TT], F32) for i in range(2)]
            ut = [self.sb(st, "ut%d" % i, [128, 8, TT], BF16) for i in range(2)]
            br = [self.sb(st, "br%d" % i, [128, 3, 4, TT], BF16) for i in range(2)]
            pinh = [self.sb(st, "pinh%d" % i, [128, 4, 512], BF16) for i in range(2)]
            yb = self.sb(st, "yb", [128, 4, 128], BF16)
            sgm = [self.sb(st, "sgm%d" % i, [128, TT], F32) for i in range(2)]
            pr = [self.sb(st, "pr%d" % i, [128, TT], F32) for i in range(3)]
            acc = self.sb(st, "acc", [128, TT], F32)
            m = self.sb(st, "m", [128, 8, TT], BF16)
            ys = self.sb(st, "ys", [128, 8, TT], F32)
            ysq = self.sb(st, "ysq", [128, 8, TT], BF16)
            rs2 = self.sb(st, "rs2", [128, TT], F32)
            if load_w:
                bw = self.load_merge_weights(l, MW)
            else:
                bw = {"wg": [P.buf("wg%d" % k) for k in range(8)], "wbr": [P.buf("wbr%d" % k) for k in range(3)],
                      "wo": [P.buf("wo0"), P.buf("wo1")], "pw": P.buf("pw"), "bands": P.buf("bands")}
            b_wg, b_wbr3, b_wo2, b_pw, b_bands = bw["wg"], bw["wbr"], bw["wo"], bw["pw"], bw["bands"]
            b_xt = [P.buf("xt0"), P.buf("xt1")]
            b_ut = [P.buf("ut0"), P.buf("ut1")]
            b_br = [[P.buf("br%d_%d" % (i, k)) for k in range(3)] for i in range(2)]
            b_pinh = [P.buf("pinh0"), P.buf("pinh1")]
            b_yb = P.buf("yb")
            b_sgm = [P.buf("sgm0"), P.buf("sgm1")]
            b_pr = [P.buf("pr%d" % i) for i in range(3)]
            b_acc, b_m, b_ys, b_ysq, b_rs2 = [P.buf(n) for n in ("acc", "m", "ys", "ysq", "rs2")]
            pscale = lambda g: self.vecs[:, 256 + l * 4 + g:256 + l * 4 + g + 1]

            def blk_info(i, blk):
                if i < 16:
                    ob = 2 * i + blk
                    return ob == 0, ob == 31
                return blk == 0, blk == 1

            def load(i):
                s = i % 2
                t0 = i * TT
                self.dma("sp", xt[s][:], d["yT"][:, t0:t0 + TT].rearrange("(k p) t -> p k t", p=128), [bd["X"]], [b_xt[s]], "xl%d" % s)
                self.dma("sp", ut[s][:], d["U"][:, t0:t0 + TT].rearrange("(k p) t -> p k t", p=128), [bd["U"]], [b_ut[s]], "ul%d" % s)
                self.dma("sp", br[s][:, 0, :, :], d["NAO"][:, t0:t0 + TT].rearrange("(c p) t -> p c t", p=128), [bd["NAO"]], [b_br[s][0]], "nl%d" % s)
                self.dma("sp", br[s][:, 2, :, :], d["MO"][:, t0:t0 + TT].rearrange("(c p) t -> p c t", p=128), [bd["MO"]], [b_br[s][2]], "ml%d" % s)
                f0, _ = blk_info(i, 0)
                _, l1 = blk_info(i, 1)
                lo = 1 if f0 else 0
                hi = 3 if l1 else 4
                r0 = t0 - 128 + lo * 128
                self.dma("sp", pinh[s][:, lo:hi, :], d["PIN"][r0:r0 + (hi - lo) * 128, :].rearrange("(b p) f -> p b f", p=128),
                         [bd["PIN"]], [b_pinh[s]], "pl%d" % s)

            load(0)
            for i in range(NTILE):
                if i + 1 < NTILE:
                    load(i + 1)
                s = i % 2
                v = 0 if i < 16 else 1
                t0 = i * TT
                for blk in range(2):
                    first, last = blk_info(i, blk)
                    pb, bpb = self.ps(0)
                    for g in range(4):
                        rels = [r for r in (-1, 0, 1) if not ((r == -1 and first) or (r == 1 and last))]
                        for ri, r in enumerate(rels):
                            kind = 0 if r == -1 else (1 if r == 1 else (3 if first else (4 if last else 2)))
                            self.mm(pb[:, g * 128:(g + 1) * 128], pinh[s][:, blk + 1 + r, g * 128:(g + 1) * 128], bands[:, g, kind, :],
                                    ri == 0, ri == len(rels) - 1, [b_pinh[s], b_bands], [bpb])
                    self.cp("act", yb[:].rearrange("p g t -> p (g t)"), pb[:, :], [bpb], [b_yb])
                    pb2, bpb2 = self.ps(1)
                    for g in range(4):
                        self.mm(pb2[:, g * 128:(g + 1) * 128], pw[:, g, :], yb[:, g, :], True, True, [b_pw, b_yb], [bpb2])
                    for g in range(4):
                        self.act(br[s][:, 1, g, blk * 128:(blk + 1) * 128], pb2[:, g * 128:(g + 1) * 128], AF.Identity, [bpb2, bc],
                                 [b_br[s][1]], scale=pscale(g))
                it = 0
                for dc in range(8):
                    for k in range(3):
                        pb, bpb = self.ps(2 + it % 2)
                        for kc in range(4):
                            self.mm(pb[:, 0:TT], wbr[:, 4 * k + kc, dc * 128:(dc + 1) * 128], br[s][:, k, kc, :], kc == 0, kc == 3,
                                    [b_wbr3[k], b_br[s][k]], [bpb])
                        for kc in range(8):
                            self.mm(pb[:, TT:2 * TT], wg[:, kc, k * 1024 + dc * 128:k * 1024 + (dc + 1) * 128], ut[s][:, kc, :], kc == 0, kc == 7,
                                    [b_wg[kc], b_ut[s]], [bpb])
                        self.act(sgm[it % 2][:], pb[:, TT:2 * TT], AF.Sigmoid, [bpb], [b_sgm[it % 2]])
                        self.tt("dve", pr[k][:], sgm[it % 2][:], pb[:, 0:TT], ALU.mult, [b_sgm[it % 2], bpb], [b_pr[k]])
                        it += 1
                    self.tt("pool", acc[:], pr[0][:], pr[1][:], ALU.add, [b_pr[0], b_pr[1]], [b_acc])
                    self.tt("pool", m[:, dc, :], acc[:], pr[2][:], ALU.add, [b_acc, b_pr[2]], [b_m])
                for oc in range(8):
                    pb, bpb = self.ps(4 + oc % 2)
                    for kc in range(8):
                        self.mm(pb[:, 0:TT], wo[:, kc, oc * 128:(oc + 1) * 128], m[:, kc, :], kc == 0, kc == 7, [b_wo2[kc // 4], b_m], [bpb])
                    self.act(ys[:, oc, :], pb[:, 0:TT], AF.Identity, [bpb, bc], [b_ys], scale=self.mod_G(l, 1, v, oc))
                    self.act(ysq[:, oc, :], pb[:, 0:TT], AF.Square, [bpb], [b_ysq])
                self.postnorm_residual(l, 1, v, ys, b_ys, ysq, b_ysq, rs2, b_rs2, xt[s], b_xt[s], 7)
                self.dma("pool", d["yT"][:, t0:t0 + TT].rearrange("(k p) t -> p k t", p=128), xt[s][:], [b_xt[s]], [bd["X"]], "xs%d" % s)
            P.emit()


_NC_CACHE = {}


def _get_nc():
    if "nc" not in _NC_CACHE:
        _NC_CACHE["nc"] = Builder().build()
    return _NC_CACHE["nc"]


def _chunk_vec(v):
    return np.ascontiguousarray(np.asarray(v, np.float32).reshape(-1, 128).T)


def kernel(x_prompt, x_sample, cache_na_k, cache_na_v, cache_mla_ckv, cache_mla_krope, c, c_ctx,
           w_ada, b_ada, norm_pre, norm_post, ffn_w_in, ffn_w_out, w_in, na_rpb, pool_w, pool_scale,
           mla_q_norm, mla_kv_norm, mla_w_uq, mla_w_ukv, w_branch, w_out):
    f32 = lambda a: np.ascontiguousarray(np.asarray(a, dtype=np.float32))
    x_prompt, x_sample = f32(x_prompt), f32(x_sample)
    w_ada, ffn_w_in, ffn_w_out, w_in = f32(w_ada), f32(ffn_w_in), f32(ffn_w_out), f32(w_in)
    w_branch, w_out, pool_w = f32(w_branch), f32(w_out), f32(pool_w)
    mla_w_uq, mla_w_ukv, na_rpb = f32(mla_w_uq), f32(mla_w_ukv), f32(na_rpb)
    w_kr = w_in[:, :, 2560:2592]
    w_kr2 = np.ascontiguousarray(np.concatenate([np.zeros((2, D, 64), np.float32), w_kr, w_kr[:, :, ROPE_PERM]], axis=2))
    uq = mla_w_uq.reshape(2, 256, 8, 96)
    w_uq2 = np.ascontiguousarray(np.concatenate([uq[..., 0:64], uq[..., 64:96], uq[..., 64:96][..., ROPE_PERM]], axis=-1).reshape(2, 256, 1024))
    ukv = mla_w_ukv.reshape(2, 256, 8, 128)
    w_ukv_k = np.ascontiguousarray(ukv[..., 0:64].reshape(2, 256, 512))
    w_ukv_v = np.ascontiguousarray(ukv[..., 64:128].reshape(2, 256, 512))
    badd = np.ascontiguousarray(_rpb_tables(na_rpb).reshape(2, 128, 16, 1536))
    bands = np.ascontiguousarray(_bands().reshape(128, 20 * 128))
    fold = _fold()
    cs = _cs_table()
    shared = {"cs": cs, "w_ada": w_ada, "ffn_w_in": ffn_w_in, "ffn_w_out": ffn_w_out, "w_in": w_in, "w_kr2": w_kr2,
              "w_uq2": w_uq2, "w_ukv_k": w_ukv_k, "w_ukv_v": w_ukv_v, "pool_w": pool_w, "w_branch": w_branch,
              "w_out": w_out, "badd": badd, "bands": bands, "fold": fold,
              "ident": np.eye(128, dtype=np.float32).astype(ml_dtypes.bfloat16)}
    vec_common = [None]
    for l in range(2):
        vec_common.append(_chunk_vec(b_ada[l]))
    vec_common.append(np.concatenate([_chunk_vec(np.asarray(norm_pre)[l, j]) for l in range(2) for j in range(3)], axis=1))
    vec_common.append(np.concatenate([_chunk_vec(np.asarray(norm_post)[l, j]) for l in range(2) for j in range(3)], axis=1))
    vec_common.append(np.concatenate([_chunk_vec(np.asarray(pool_scale)[l]) for l in range(2)], axis=1))
    vec_common.append(np.concatenate([_chunk_vec(np.asarray(mla_q_norm)[l]) for l in range(2)], axis=1))
    vec_common.append(np.concatenate([_chunk_vec(np.asarray(mla_kv_norm)[l]) for l in range(2)], axis=1))
    c = f32(c)
    c_ctx = f32(c_ctx)
    in_maps = []
    for i in range(8):
        xs = x_sample[i].T
        xp = x_prompt[4 * i:4 * i + 4].reshape(NPR, D).T
        xT = np.ascontiguousarray(np.concatenate([xs, xp], axis=1))
        c2 = np.stack([_chunk_vec(c[i]), _chunk_vec(c_ctx)], axis=2).reshape(128, 16)
        vecs = np.ascontiguousarray(np.concatenate([c2] + vec_common[1:], axis=1).astype(np.float32))
        assert vecs.shape == (128, NVEC), vecs.shape
        m = dict(shared)
        m["xT"] = xT
        m["vecs"] = vecs
        m["c_kT"] = np.ascontiguousarray(f32(cache_na_k[i]).reshape(2, 256, 512).transpose(0, 2, 1))
        m["c_v"] = np.ascontiguousarray(f32(cache_na_v[i]).reshape(2, 256, 512))
        m["c_ckvT"] = np.ascontiguousarray(f32(cache_mla_ckv[i]).transpose(0, 2, 1))
        m["c_krT"] = np.ascontiguousarray(f32(cache_mla_krope[i]).transpose(0, 2, 1))
        in_maps.append(m)
    nc = _get_nc()
    res = run_bass_kernel_spmd(nc, in_maps, core_ids=list(range(8)))
    rs = res.results
    y_prompt = np.empty((32, 256, D), np.float32)
    y_sample = np.empty((8, NS, D), np.float32)
    st_k = np.empty((32, 2, 256, 8, 64), np.float32)
    st_v = np.empty((32, 2, 256, 8, 64), np.float32)
    st_ckv = np.empty((32, 2, 256, 256), np.float32)
    st_kr = np.empty((32, 2, 256, 32), np.float32)
    for i in range(8):
        r = rs[i]
        yT = np.asarray(r["yT"])
        y_sample[i] = yT[:, :NS].T
        y_prompt[4 * i:4 * i + 4] = yT[:, NS:].T.reshape(4, 256, D)
        k = np.asarray(r["st_k"])
        st_k[4 * i:4 * i + 4] = k.transpose(2, 0, 1).reshape(4, 256, 2, 8, 64).transpose(0, 2, 1, 3, 4)
        vv = np.asarray(r["st_v"])
        st_v[4 * i:4 * i + 4] = vv.reshape(2, 4, 256, 8, 64).transpose(1, 0, 2, 3, 4)
        ck = np.asarray(r["st_ckv"])
        st_ckv[4 * i:4 * i + 4] = ck.transpose(2, 0, 1).reshape(4, 256, 2, 256).transpose(0, 2, 1, 3)
        kr = np.asarray(r["st_kr"])
        st_kr[4 * i:4 * i + 4] = kr.transpose(2, 0, 1).reshape(4, 256, 2, 32).transpose(0, 2, 1, 3)
    return (y_prompt, y_sample, st_k, st_v, st_ckv, st_kr)
```
